# Optimizing a Trainium2 kernel written in Bass

```python
import math
import jax
import jax.numpy as jnp
from jax import lax
import numpy as np

D_MODEL = 1024
BATCH = 16
SEQ = 2048
DEPTH = 1

GRID_W = 64
CTX_LEN = 256
MIX_WIDTH = D_MODEL
RET_WIDTH = MIX_WIDTH // 2
RET_HEADS = 4
RET_HEAD_DIM = RET_WIDTH // RET_HEADS
RET_CHUNK = 128
RWKV_WIDTH = MIX_WIDTH - RET_WIDTH
RWKV_HEAD_DIM = 64
RWKV_HEADS = RWKV_WIDTH // RWKV_HEAD_DIM
DECAY_LORA = 64
AAA_LORA = 64
GATE_LORA = 128
D_FF = 4 * D_MODEL
ROPE_BASE = 10000.0
NORM_EPS = 1e-6
GN_EPS = 64e-5
W_DECAY_SCALE = math.exp(-0.5)
RET_COLS = 4 * RET_WIDTH
SHIFT_COLS = 3 * RWKV_WIDTH + DECAY_LORA + AAA_LORA + GATE_LORA
IN_COLS = RET_COLS + SHIFT_COLS

kernel_name = "hybrid_retention_rwkv7_dit_layer"


def rmsnorm(x, g):
    xf = x.astype(jnp.float32)
    y = xf * lax.rsqrt(jnp.mean(xf * xf, axis=-1, keepdims=True) + NORM_EPS)
    return (y * g.astype(jnp.float32)).astype(x.dtype)


def adaln_params(cvec, w_ada, b_ada):
    m = jax.nn.silu(cvec) @ w_ada + b_ada
    return jnp.split(m, 6, axis=-1)


def modulate(h, shift, scale):
    return h * (1 + scale) + shift


def flip_t(a):
    return jnp.flip(a, axis=1)


def split_heads(t, n_heads, head_dim):
    return t.reshape(t.shape[0], t.shape[1], n_heads, head_dim).astype(jnp.float32)


def rope_tables(rows, cols):
    half = RET_HEAD_DIM // 2
    inv = jnp.power(ROPE_BASE, -jnp.arange(0, half, 2, dtype=jnp.float32) / half)
    ang_r = rows.astype(jnp.float32)[:, None] * inv[None, :]
    ang_c = cols.astype(jnp.float32)[:, None] * inv[None, :]
    return (jnp.cos(ang_r), jnp.sin(ang_r), jnp.cos(ang_c), jnp.sin(ang_c))


def rotate_block(x, cos, sin):
    x1, x2 = jnp.split(x, 2, axis=-1)
    cos = cos[None, :, None, :]
    sin = sin[None, :, None, :]
    return jnp.concatenate([x1 * cos - x2 * sin, x1 * sin + x2 * cos], axis=-1)


def apply_rope_2d(x, tables):
    cr, sr, cc, sc = tables
    half = x.shape[-1] // 2
    return jnp.concatenate([rotate_block(x[..., :half], cr, sr), rotate_block(x[..., half:], cc, sc)], axis=-1)


def retention_scan(q, k, v, log_gamma, s0, inclusive):
    bsz, t_len, n_h, _ = q.shape
    dv = v.shape[-1]
    c = RET_CHUNK
    n_chunks = t_len // c

    def chunks(a):
        return a.reshape(bsz, n_chunks, c, n_h, a.shape[-1]).transpose(1, 0, 3, 2, 4)

    idx = jnp.arange(c, dtype=jnp.float32)
    dist = idx[:, None] - idx[None, :]
    mask = (dist >= 0) if inclusive else (dist > 0)
    lg = log_gamma[:, None, None]
    intra_decay = jnp.where(mask[None], jnp.exp(lg * jnp.maximum(dist, 0.0)[None]), 0.0)
    q_decay = jnp.exp(log_gamma[:, None] * (idx + 1.0)[None, :])
    k_decay = jnp.exp(log_gamma[:, None] * (c - 1.0 - idx)[None, :])
    chunk_decay = jnp.exp(log_gamma * c)

    def body(state, inp):
        qc, kc, vc = inp
        scores = jnp.einsum('bhid,bhjd->bhij', qc, kc) * intra_decay
        out = (jnp.einsum('bhij,bhjd->bhid', scores, vc)
               + jnp.einsum('bhid,bhde->bhie', qc * q_decay[..., None], state))
        state = (state * chunk_decay[:, None, None]
                 + jnp.einsum('bhjd,bhje->bhde', kc * k_decay[..., None], vc))
        return state, out

    s_final, outs = lax.scan(body, s0, (chunks(q), chunks(k), chunks(v)))
    out = outs.transpose(1, 0, 3, 2, 4).reshape(bsz, t_len, n_h, dv)
    return out, s_final


def retention_bidir(q, k, v, log_g, s0_fwd, s0_bwd):
    o_f, s_f = retention_scan(q, k, v, log_g[0], s0_fwd, True)
    o_b, s_b = retention_scan(flip_t(q), flip_t(k), flip_t(v), log_g[1], s0_bwd, False)
    return o_f + flip_t(o_b), s_f, s_b


def head_rms(o):
    o = o * lax.rsqrt(jnp.mean(o * o, axis=-1, keepdims=True) + NORM_EPS)
    return o.reshape(o.shape[0], o.shape[1], -1)


def token_shift(p, mu):
    prev = jnp.pad(p[:, :-1], ((0, 0), (1, 0), (0, 0)))
    nxt = jnp.pad(p[:, 1:], ((0, 0), (0, 1), (0, 0)))
    return p + mu[0] * (prev - p) + mu[1] * (nxt - p)


def rwkv_prepare(p, shift_mu, w0, w_up, a0, a_up, g_up, k_k, k_a):
    p = token_shift(p, shift_mu)
    w_ = RWKV_WIDTH
    r, k, v, wl, al, gl = jnp.split(
        p, [w_, 2 * w_, 3 * w_, 3 * w_ + DECAY_LORA, 3 * w_ + DECAY_LORA + AAA_LORA], axis=-1)
    kk = split_heads(k * k_k, RWKV_HEADS, RWKV_HEAD_DIM)
    kk = kk * lax.rsqrt(jnp.sum(kk * kk, axis=-1, keepdims=True) + 1e-12)
    dirs = []
    for d in range(2):
        w = jnp.exp(-W_DECAY_SCALE * jax.nn.sigmoid((w0[d] + jnp.tanh(wl) @ w_up[d]).astype(jnp.float32)))
        a = jax.nn.sigmoid((a0[d] + al @ a_up[d]).astype(jnp.float32))
        kt = k.astype(jnp.float32) * (1.0 + (a - 1.0) * k_a.astype(jnp.float32))
        dirs.append((split_heads(w, RWKV_HEADS, RWKV_HEAD_DIM),
                     split_heads(a, RWKV_HEADS, RWKV_HEAD_DIM),
                     split_heads(kt, RWKV_HEADS, RWKV_HEAD_DIM)))
    g = jax.nn.sigmoid(gl) @ g_up
    return (split_heads(r, RWKV_HEADS, RWKV_HEAD_DIM), split_heads(v, RWKV_HEADS, RWKV_HEAD_DIM), kk, dirs, g)


def rwkv7_scan(r, w, kk, a, kt, v, s0, inclusive):
    def update(state, w_t, kk_t, a_t, kt_t, v_t):
        removed = jnp.einsum('bhvk,bhk->bhv', state, kk_t)
        return (state * w_t[:, :, None, :]
                - removed[..., None] * (kk_t * a_t)[:, :, None, :]
                + v_t[..., None] * kt_t[:, :, None, :])

    def body(state, inp):
        r_t, w_t, kk_t, a_t, kt_t, v_t = inp
        if inclusive:
            state = update(state, w_t, kk_t, a_t, kt_t, v_t)
            y = jnp.einsum('bhvk,bhk->bhv', state, r_t)
        else:
            y = jnp.einsum('bhvk,bhk->bhv', state, r_t)
            state = update(state, w_t, kk_t, a_t, kt_t, v_t)
        return state, y

    xs = tuple(jnp.moveaxis(t, 1, 0) for t in (r, w, kk, a, kt, v))
    s_final, ys = lax.scan(body, s0, xs)
    return jnp.moveaxis(ys, 0, 1), s_final


def rwkv_bidir(r, v, kk, dirs, s0_fwd, s0_bwd):
    (w_f, a_f, kt_f), (w_b, a_b, kt_b) = dirs
    y_f, s_f = rwkv7_scan(r, w_f, kk, a_f, kt_f, v, s0_fwd, True)
    y_b, s_b = rwkv7_scan(flip_t(r), flip_t(w_b), flip_t(kk), flip_t(a_b), flip_t(kt_b), flip_t(v), s0_bwd, False)
    return y_f + flip_t(y_b), s_f, s_b


def merge_heads(o_ret, g_ret, y_rw, feat, r_k, ln_w, ln_b, w_out, dtype):
    r, v, _, dirs, g_rw = feat
    ret_out = head_rms(o_ret) * jax.nn.silu(g_ret.astype(jnp.float32))
    mean = jnp.mean(y_rw, axis=-1, keepdims=True)
    var = jnp.var(y_rw, axis=-1, keepdims=True)
    y_n = ((y_rw - mean) * lax.rsqrt(var + GN_EPS)).reshape(y_rw.shape[0], y_rw.shape[1], -1)
    y_n = y_n * ln_w.astype(jnp.float32) + ln_b.astype(jnp.float32)
    kt_f = dirs[0][2]
    rk = r_k.astype(jnp.float32).reshape(RWKV_HEADS, RWKV_HEAD_DIM)
    bonus = (jnp.sum(r * kt_f * rk, axis=-1, keepdims=True) * v).reshape(y_n.shape)
    rw_out = (y_n + bonus) * g_rw.astype(jnp.float32)
    return jnp.concatenate([ret_out, rw_out], axis=-1).astype(dtype) @ w_out


def token_mixers(hx, hc, rope, w_in, log_decay, shift_mu, w0, w_up, a0, a_up, g_up,
                 k_k, k_a, r_k, ln_w, ln_b, w_out, with_ctx_out):
    bsz = hx.shape[0]
    px = hx @ w_in
    pc = hc @ w_in

    log_g = -jnp.exp(log_decay.astype(jnp.float32))
    k_scale = RET_HEAD_DIM ** -0.5

    def ret_qkvg(p):
        q, k, v, g = jnp.split(p[..., :RET_COLS], 4, axis=-1)
        return (split_heads(q, RET_HEADS, RET_HEAD_DIM), split_heads(k, RET_HEADS, RET_HEAD_DIM) * k_scale,
                split_heads(v, RET_HEADS, RET_HEAD_DIM), g)

    qc, kc, vc, gc = ret_qkvg(pc)
    qx, kx, vx, gx = ret_qkvg(px)
    qx = apply_rope_2d(qx, rope)
    kx = apply_rope_2d(kx, rope)
    zeros_ret = jnp.zeros((bsz, RET_HEADS, RET_HEAD_DIM, RET_HEAD_DIM), jnp.float32)
    oc_ret, sf_ret, sb_ret = retention_bidir(qc, kc, vc, log_g, zeros_ret, zeros_ret)
    ox_ret, _, _ = retention_bidir(qx, kx, vx, log_g, sf_ret, sb_ret)

    feat_c = rwkv_prepare(pc[..., RET_COLS:], shift_mu, w0, w_up, a0, a_up, g_up, k_k, k_a)
    feat_x = rwkv_prepare(px[..., RET_COLS:], shift_mu, w0, w_up, a0, a_up, g_up, k_k, k_a)
    zeros_rw = jnp.zeros((bsz, RWKV_HEADS, RWKV_HEAD_DIM, RWKV_HEAD_DIM), jnp.float32)
    yc_rw, sf_rw, sb_rw = rwkv_bidir(feat_c[0], feat_c[1], feat_c[2], feat_c[3], zeros_rw, zeros_rw)
    yx_rw, _, _ = rwkv_bidir(feat_x[0], feat_x[1], feat_x[2], feat_x[3], sf_rw, sb_rw)

    out_x = merge_heads(ox_ret, gx, yx_rw, feat_x, r_k, ln_w, ln_b, w_out, hx.dtype)
    out_c = merge_heads(oc_ret, gc, yc_rw, feat_c, r_k, ln_w, ln_b, w_out, hc.dtype) if with_ctx_out else None
    return out_x, out_c


def squared_relu_mlp(h, w1, b1, w2, b2):
    return jnp.square(jax.nn.relu(h @ w1 + b1)) @ w2 + b2


def setup_inputs(seed: int = 0) -> dict:
    key = jax.random.key(seed)
    ks = jax.random.split(key, 32)
    L, D, W = DEPTH, D_MODEL, RWKV_WIDTH

    def nrm(k, shape, s):
        return jax.random.normal(k, shape, jnp.float32) * s

    base_decay = jnp.log(-jnp.log(1.0 - jnp.power(2.0, -5.0 - jnp.arange(RET_HEADS, dtype=jnp.float32))))
    return {
        'x': nrm(ks[0], (BATCH, SEQ, D), 1.0),
        'c': nrm(ks[1], (BATCH, D), 1.0),
        'ctx': nrm(ks[2], (BATCH, CTX_LEN, D), 1.0),
        'c_ctx': nrm(ks[3], (D,), 1.0),
        'w_ada': nrm(ks[4], (L, D, 6 * D), D ** -0.5),
        'b_ada': nrm(ks[5], (L, 6 * D), 0.02),
        'norm1_g': 1.0 + nrm(ks[6], (L, D), 0.02),
        'norm2_g': 1.0 + nrm(ks[7], (L, D), 0.02),
        'w_in': nrm(ks[8], (L, D, IN_COLS), D ** -0.5),
        'ret_log_decay': base_decay + nrm(ks[9], (L, 2, RET_HEADS), 0.05),
        'rwkv_shift_mu': jax.random.uniform(ks[10], (L, 2, SHIFT_COLS), jnp.float32, 0.0, 0.5),
        'rwkv_w0': jax.random.uniform(ks[11], (L, 2, W), jnp.float32, -3.0, 1.0),
        'rwkv_w_up': nrm(ks[12], (L, 2, DECAY_LORA, W), 0.5 * DECAY_LORA ** -0.5),
        'rwkv_a0': nrm(ks[13], (L, 2, W), 0.5),
        'rwkv_a_up': nrm(ks[14], (L, 2, AAA_LORA, W), 0.5 * AAA_LORA ** -0.5),
        'rwkv_g_up': nrm(ks[15], (L, GATE_LORA, W), GATE_LORA ** -0.5),
        'rwkv_k_k': 0.85 + nrm(ks[16], (L, W), 0.05),
        'rwkv_k_a': 1.0 + nrm(ks[17], (L, W), 0.05),
        'rwkv_r_k': nrm(ks[18], (L, W), 0.1),
        'rwkv_ln_w': 1.0 + nrm(ks[19], (L, W), 0.02),
        'rwkv_ln_b': nrm(ks[20], (L, W), 0.02),
        'w_out': nrm(ks[21], (L, D, D), D ** -0.5),
        'w_ff1': nrm(ks[22], (L, D, D_FF), D ** -0.5),
        'b_ff1': nrm(ks[23], (L, D_FF), 0.02),
        'w_ff2': nrm(ks[24], (L, D_FF, D), D_FF ** -0.5),
        'b_ff2': nrm(ks[25], (L, D), 0.02),
        'final_g': 1.0 + nrm(ks[26], (D,), 0.02),
    }


def reference(x, c, ctx, c_ctx, w_ada, b_ada, norm1_g, norm2_g, w_in, ret_log_decay,
              rwkv_shift_mu, rwkv_w0, rwkv_w_up, rwkv_a0, rwkv_a_up, rwkv_g_up,
              rwkv_k_k, rwkv_k_a, rwkv_r_k, rwkv_ln_w, rwkv_ln_b, w_out,
              w_ff1, b_ff1, w_ff2, b_ff2, final_g):
    n_tokens = x.shape[1]
    ROWS = n_tokens // GRID_W
    rows = jnp.repeat(jnp.arange(ROWS), GRID_W)
    cols = jnp.tile(jnp.arange(GRID_W), ROWS)
    rope = rope_tables(rows, cols)

    h_x, h_c = x, ctx
    for l in range(DEPTH):
        with_ctx = l < DEPTH - 1
        sh1, sc1, g1, sh2, sc2, g2 = [m[:, None, :] for m in adaln_params(c, w_ada[l], b_ada[l])]
        csh1, csc1, cg1, csh2, csc2, cg2 = adaln_params(c_ctx, w_ada[l], b_ada[l])
        nx = modulate(rmsnorm(h_x, norm1_g[l]), sh1, sc1)
        nc = modulate(rmsnorm(h_c, norm1_g[l]), csh1, csc1)
        mix_x, mix_c = token_mixers(nx, nc, rope, w_in[l], ret_log_decay[l], rwkv_shift_mu[l],
                                    rwkv_w0[l], rwkv_w_up[l], rwkv_a0[l], rwkv_a_up[l], rwkv_g_up[l],
                                    rwkv_k_k[l], rwkv_k_a[l], rwkv_r_k[l], rwkv_ln_w[l], rwkv_ln_b[l],
                                    w_out[l], with_ctx)
        h_x = h_x + g1 * mix_x
        h_x = h_x + g2 * squared_relu_mlp(modulate(rmsnorm(h_x, norm2_g[l]), sh2, sc2),
                                          w_ff1[l], b_ff1[l], w_ff2[l], b_ff2[l])
        if with_ctx:
            h_c = h_c + cg1 * mix_c
            h_c = h_c + cg2 * squared_relu_mlp(modulate(rmsnorm(h_c, norm2_g[l]), csh2, csc2),
                                              w_ff1[l], b_ff1[l], w_ff2[l], b_ff2[l])
    return rmsnorm(h_x, final_g)
```

```python
import math
import numpy as np
from contextlib import ExitStack
import concourse.bass as bass
import concourse.mybir as mybir
from concourse.bass_utils import run_bass_kernel_spmd

F32 = mybir.dt.float32
BF16 = mybir.dt.bfloat16
AF = mybir.ActivationFunctionType
ALU = mybir.AluOpType
AX = mybir.AxisListType

T = 2304
NCH = 18
D = 1024
SW = math.exp(-0.5)
NSLOT = 24


class _Stop(Exception):
    pass


class Dep:
    __slots__ = ("w", "r", "excl")

    def __init__(self):
        self.w = None
        self.r = {}
        self.excl = False


class Tl:
    def __init__(self, t):
        self.t = t
        self.d = Dep()
        self.k = {}

    def __getitem__(self, i):
        return self.t[i]

    def s(self, key):
        if key not in self.k:
            self.k[key] = Dep()
        return self.k[key]


def _deps(lst):
    out = []
    for x in lst:
        if isinstance(x, Tl):
            out.append(x.d)
        elif isinstance(x, Dep):
            out.append(x)
        elif isinstance(x, (list, tuple)):
            out.extend(_deps(x))
        elif x is None:
            pass
        else:
            raise TypeError(type(x))
    return out


class Sched:
    def __init__(self, nc, ES):
        self.nc = nc
        self.E = {}
        self.sems = {}
        for nm, obj in (("pe", nc.tensor), ("act", nc.scalar), ("dve", nc.vector),
                        ("pool", nc.gpsimd), ("sp", nc.sync)):
            sem = ES.enter_context(nc.semaphore("s_" + nm))
            self.E[nm] = dict(o=obj, sem=sem, cnt=0, waited={})
            self.sems[nm] = sem
        self.slots = []
        for i in range(NSLOT):
            sem = ES.enter_context(nc.semaphore("dq%d" % i))
            self.sems["dq%d" % i] = sem
            self.slots.append(["dq%d" % i, 0])
        self.rr = 0
        self.nins = 0

    def _wait(self, eng, needs):
        e = self.E[eng]
        for k, c in needs.items():
            if e["waited"].get(k, 0) < c:
                e["o"].wait_ge(self.sems[k], c)
                e["waited"][k] = c

    def _needs(self, eng, R, W):
        needs = {}

        def add(tok, raw):
            if tok is None:
                return
            k, c = tok
            if k == eng and eng == "pe":
                return
            if needs.get(k, 0) < c:
                needs[k] = c

        for d in R:
            add(d.w, True)
            if d.excl:
                for k, c in d.r.items():
                    add((k, c), False)
        for d in W:
            add(d.w, False)
            for k, c in d.r.items():
                add((k, c), False)
        return needs

    def _commit(self, tok, R, W):
        k, c = tok
        for d in R:
            if d.r.get(k, 0) < c:
                d.r[k] = c
        for d in W:
            d.w = tok
            d.r = {}

    def op(self, eng, fn, R, W, **kw):
        R = _deps(R)
        W = _deps(W)
        self._wait(eng, self._needs(eng, R, W))
        e = self.E[eng]
        ins = getattr(e["o"], fn)(**kw)
        e["cnt"] += 1
        ins.then_inc(e["sem"], 1)
        self._commit((eng, e["cnt"]), R, W)
        self.nins += 1

    def dma(self, out, in_, R, W, q="sp"):
        R = _deps(R)
        W = _deps(W)
        needs = self._needs(q, R, W)
        slot = self.slots[self.rr]
        self.rr = (self.rr + 1) % len(self.slots)
        if slot[1] > 0:
            needs[slot[0]] = max(needs.get(slot[0], 0), slot[1] * 16)
        self._wait(q, needs)
        self.E[q]["o"].dma_start(out=out, in_=in_).then_inc(self.sems[slot[0]], 16)
        slot[1] += 1
        self._commit((slot[0], slot[1] * 16), R, W)
        self.nins += 1

    def barrier(self):
        needs = {nm: e["cnt"] for nm, e in self.E.items() if e["cnt"] > 0}
        for sl in self.slots:
            if sl[1] > 0:
                needs[sl[0]] = sl[1] * 16
        for nm in self.E:
            n2 = {k: v for k, v in needs.items() if k != nm}
            self._wait(nm, n2)

    def finish(self):
        needs = {s[0]: s[1] * 16 for s in self.slots if s[1] > 0}
        self._wait("sp", needs)


CST = {}
_off = 0
for _n, _w in (("ident", 128), ("bones", 128), ("distF", 128), ("distB", 128), ("maskF", 128), ("maskB", 128),
               ("iota1", 128), ("iotab", 128), ("colF", 1), ("colB", 1), ("bcols", 2), ("ones", 128),
               ("mk1f", 256), ("mk2f", 256), ("mk3f", 128), ("mk1b", 256), ("mk2b", 256), ("mk3b", 128),
               ("epsn", 1), ("epsg", 1), ("epsk", 1), ("blk16", 128), ("o16", 128), ("o32", 128), ("o64", 128), ("ident2", 256), ("cv3", 3)):
    CST[_n] = (_off, _w)
    _off += _w
NCST = _off


def _consts():
    c = np.zeros((128, NCST), np.float32)

    def put(n, a):
        o, w = CST[n]
        c[:, o:o + w] = np.asarray(a, np.float32).reshape(128, w)

    i = np.arange(128)
    a = i[:, None]
    b = i[None, :]
    put("ident", np.eye(128))
    put("bones", (a // 64) == (b // 64))
    put("distF", np.maximum(b - a, 0))
    put("distB", np.maximum(a - b, 0))
    put("maskF", b >= a)
    put("maskB", a > b)
    put("iota1", np.broadcast_to(b + 1, (128, 128)))
    put("iotab", np.broadcast_to(128 - b, (128, 128)))
    put("colF", 127 - i)
    put("colB", i)
    put("bcols", np.stack([(i < 64), (i >= 64)], 1))
    put("ones", np.ones((128, 128)))
    Us = (a < b).astype(np.float32)
    Ui = (a <= b).astype(np.float32)
    Ls = (a > b).astype(np.float32)
    put("mk1f", np.concatenate([-Us, Ui], 1))
    put("mk2f", np.concatenate([Us, Ui], 1))
    put("mk3f", -Ls)
    put("mk1b", np.concatenate([-Ls, Ls], 1))
    put("mk2b", np.concatenate([Ls, Ls], 1))
    put("mk3b", -Us)
    put("epsn", np.full(128, 1e-6))
    put("epsg", np.full(128, 64e-5))
    put("epsk", np.full(128, 1e-12))
    blk = lambda n: (a // n) == (b // n)
    put("blk16", blk(16))
    put("o16", blk(32) & ~blk(16))
    put("o32", blk(64) & ~blk(32))
    put("o64", ~blk(64))
    put("ident2", np.concatenate([np.eye(128), np.eye(128)], 1))
    put("cv3", np.broadcast_to(np.array([0.5 * SW, -0.5 * SW, -SW]), (128, 3)))
    return c


def _rope_tables():
    half = 64
    inv = np.power(np.float32(10000.0), -np.arange(0, half, 2, dtype=np.float32) / np.float32(half)).astype(np.float32)
    t = np.arange(2048)
    rows = (t // 64).astype(np.float32)
    cols = (t % 64).astype(np.float32)
    cosT = np.ones((128, T), np.float32)
    sinT = np.zeros((128, T), np.float32)
    for p in range(128):
        f = p % 32
        pos = rows if p < 64 else cols
        ang = (pos * inv[f]).astype(np.float32)
        sgn = -1.0 if (p % 64) < 32 else 1.0
        cosT[p, 256:] = np.cos(ang)
        sinT[p, 256:] = sgn * np.sin(ang)
    return cosT, sinT


def _partner_perm():
    p = np.arange(128)
    return np.where((p % 64) < 32, p + 32, p - 32)


def build(phases=("mix", "ffn"), stop=None, taps=()):
    nc = bass.Bass("TRN2", target_bir_lowering=False)
    ES = ExitStack()

    def dram(name, shape, dt=F32, kind="ExternalInput"):
        return nc.dram_tensor(name, list(shape), dt, kind=kind).ap()

    xs = dram("xs", [2, T, D])
    cT = dram("cT", [128, 8, 3])
    w_ada = dram("w_ada", [D, 6144])
    b_adaT = dram("b_adaT", [128, 48])
    n1g = dram("n1g", [128, 8])
    n2g = dram("n2g", [128, 8])
    w_in = dram("w_in", [D, 3840])
    w_perm = dram("w_perm", [D, 1024])
    ldec = dram("ldec", [128, 8])
    mu = dram("mu", [128, 14, 2])
    w0T = dram("w0T", [128, 4, 2])
    a0T = dram("a0T", [128, 4, 2])
    w_up = dram("w_up", [2, 64, 512])
    a_up = dram("a_up", [2, 64, 512])
    g_up = dram("g_up", [128, 512])
    k_kT = dram("k_kT", [128, 4])
    k_aT = dram("k_aT", [128, 4])
    r_kT = dram("r_kT", [128, 4])
    lnw = dram("lnw", [128, 512])
    lnb = dram("lnb", [128, 512])
    w_out = dram("w_out", [D, D])
    w_ff1 = dram("w_ff1", [D, 4096])
    b1T = dram("b1T", [128, 32])
    w_ff2 = dram("w_ff2", [4096, D])
    b2bc = dram("b2bc", [128, D])
    fgbc = dram("fgbc", [128, D])
    cst_d = dram("cst", [128, NCST])
    cos_d = dram("cosT", [128, T])
    sin_d = dram("sinT", [128, T])
    out = dram("out", [2, 2048, D], kind="ExternalOutput")
    mixs = dram("mixs", [2, 8, 128, 2048], BF16, kind="ExternalOutput")
    out_dep = [[Dep() for _ in range(16)] for _ in range(2)]
    mix_dep = [[Dep() for _ in range(16)] for _ in range(2)]

    S = Sched(nc, ES)
    cnt = [0]

    def tap(name, src, shape, R, dt=F32):
        if name in taps:
            dtn = dram('tap_' + name, shape, dt, kind='ExternalOutput')
            S.dma(dtn, src, R, [Dep()])

    def chk(name):
        if stop == name:
            S.finish()
            nc._nins = S.nins
            raise _Stop(nc)

    def sb(ES_, shape, dt=F32, name=None):
        cnt[0] += 1
        return Tl(ES_.enter_context(nc.sbuf_tensor("t%d_%s" % (cnt[0], name or ""), list(shape), dt)))

    PS = [Tl(ES.enter_context(nc.psum_tensor("ps%d" % i, [128, 512], F32))) for i in range(8)]
    for p_ in PS:
        p_.d.excl = True
    pidx = [0]

    def P():
        p = PS[pidx[0]]
        pidx[0] = (pidx[0] + 1) % 8
        return p

    def V(fn, R, W, **kw):
        S.op("dve", fn, R, W, **kw)

    def A(fn, R, W, **kw):
        S.op("act", fn, R, W, **kw)

    def G(fn, R, W, **kw):
        S.op("pool", fn, R, W, **kw)

    def MM(R, W, **kw):
        S.op("pe", "matmul", R, W, **kw)

    def TR(R, W, **kw):
        S.op("pe", "transpose", R, W, **kw)

    cst = sb(ES, [128, NCST], name="cst")
    S.dma(cst[:], cst_d[:, :], [], [cst])

    def C(n):
        o, w = CST[n]
        return cst.t[:, o:o + w]

    ident = C("ident")
    modT = sb(ES, [128, 48, 3], name="modT")
    A1 = sb(ES, [128, 8, 3], name="A1")
    A2 = sb(ES, [128, 8, 3], name="A2")
    EG = ExitStack()
    wst = [sb(EG, [128, 8, 128], name="wst%d" % i) for i in range(2)]
    wsti = [0]
    wbf = [sb(EG, [128, 8, 128], BF16, name="wbf%d" % i) for i in range(3)]
    wbfi = [0]

    def load_cols(src, c0, n=128, cast_eng="pool"):
        st = wst[wsti[0] % 2]
        wsti[0] += 1
        S.dma(st.t[:, :, 0:n], src.rearrange("(kc p) c -> p kc c", p=128)[:, :, c0:c0 + n], [], [st])
        wb = wbf[wbfi[0] % len(wbf)]
        wbfi[0] += 1
        S.op(cast_eng, "tensor_copy", [st], [wb], out=wb.t[:, :, 0:n], in_=st.t[:, :, 0:n])
        return wb

    with ExitStack() as E0:
        ct = sb(E0, [128, 8, 3], name="ct")
        silc = sb(E0, [128, 8, 3], name="silc")
        bad = sb(E0, [128, 48], name="bad")
        g1n = sb(E0, [128, 8], name="g1n")
        g2n = sb(E0, [128, 8], name="g2n")
        S.dma(ct[:], cT[:, :, :], [], [ct])
        S.dma(bad[:], b_adaT[:, :], [], [bad])
        S.dma(g1n[:], n1g[:, :], [], [g1n])
        S.dma(g2n[:], n2g[:, :], [], [g2n])
        A("activation", [ct], [silc], out=silc[:], in_=ct[:], func=AF.Silu)
        for oc in range(48):
            st = wst[wsti[0] % 2]
            wsti[0] += 1
            S.dma(st[:], w_ada.rearrange("(kc p) c -> p kc c", p=128)[:, :, oc * 128:(oc + 1) * 128], [], [st])
            ps = P()
            for kc in range(8):
                MM([st, silc], [ps], out=ps.t[:, 0:3], lhsT=st.t[:, kc, :], rhs=silc.t[:, kc, :],
                   start=(kc == 0), stop=(kc == 7))
            V("tensor_scalar", [ps, bad], [modT], out=modT.t[:, oc, :], in0=ps.t[:, 0:3],
              scalar1=bad.t[:, oc:oc + 1], scalar2=None, op0=ALU.add)
        for kc in range(8):
            V("tensor_scalar", [modT, g1n], [A1], out=A1.t[:, kc, :], in0=modT.t[:, 8 + kc, :], scalar1=1.0,
              scalar2=g1n.t[:, kc:kc + 1], op0=ALU.add, op1=ALU.mult)
            V("tensor_scalar", [modT, g2n], [A2], out=A2.t[:, kc, :], in0=modT.t[:, 32 + kc, :], scalar1=1.0,
              scalar2=g2n.t[:, kc:kc + 1], op0=ALU.add, op1=ALU.mult)

    tap('modT', modT[:], [128, 48, 3], [modT])
    tap('A1', A1[:], [128, 8, 3], [A1])

    def bcast_row(ES_, base, j, dst):
        dg = sb(ES_, [128, 128], name="dg")
        for half in range(2):
            ps = P()
            for q in range(4):
                kc = half * 4 + q
                V("tensor_scalar", [modT], [dg], out=dg[:], in0=ident, scalar1=modT.t[:, base + kc, j:j + 1],
                  scalar2=None, op0=ALU.mult)
                MM([dg], [ps], out=ps.t[:, q * 128:(q + 1) * 128], lhsT=C("ones"), rhs=dg[:], start=True, stop=True)
            A("activation", [ps], [dst], out=dst.t[:, half * 512:(half + 1) * 512], in_=ps.t[:, :], func=AF.Copy)

    def rms_rstd(ES_, xt, junk, ss, sd, rstd, width, eps_name):
        A("activation", [xt], [junk], out=junk.t[:, 0:width], in_=xt.t[:, 0:width], func=AF.Square)
        V("tensor_reduce", [junk], [ss], out=ss[:], in_=junk.t[:, 0:width], axis=AX.X, op=ALU.add)
        A("activation", [ss], [sd], out=sd[:], in_=ss[:], func=AF.Sqrt, scale=1.0 / width, bias=C(eps_name))
        V("reciprocal", [sd], [rstd], out=rstd[:], in_=sd[:])

    if "mix" in phases:
      with ExitStack() as EM:
        nxT = sb(EM, [128, 8, T], BF16, name="nxT")
        lnw_t = sb(EM, [128, 512], name="lnw")
        lnb_t = sb(EM, [128, 512], name="lnb")
        S.dma(lnw_t[:], lnw[:, :], [], [lnw_t])
        S.dma(lnb_t[:], lnb[:, :], [], [lnb_t])
        small = sb(EM, [128, 64], name="small")
        S.dma(small.t[:, 0:8], ldec[:, :], [], [small])
        S.dma(small.t[:, 8:12], k_kT[:, :], [], [small])
        S.dma(small.t[:, 12:16], k_aT[:, :], [], [small])
        S.dma(small.t[:, 16:20], r_kT[:, :], [], [small])
        S.dma(small.t[:, 20:28], w0T.rearrange("p a b -> p (a b)"), [], [small])
        S.dma(small.t[:, 28:36], a0T.rearrange("p a b -> p (a b)"), [], [small])
        mu_t = sb(EM, [128, 14, 2], name="mu")
        mu0 = sb(EM, [128, 14], name="mu0")
        S.dma(mu_t[:], mu[:, :, :], [], [mu_t])
        V("tensor_tensor", [mu_t], [mu0], out=mu0[:], in0=mu_t.t[:, :, 0], in1=mu_t.t[:, :, 1], op=ALU.add)
        V("tensor_scalar", [mu0], [mu0], out=mu0[:], in0=mu0[:], scalar1=-1.0, scalar2=1.0, op0=ALU.mult, op1=ALU.add)
        lgs = sb(EM, [128, 8], name="lgs")
        g128 = sb(EM, [128, 8], name="g128")
        A("activation", [small], [lgs], out=lgs[:], in_=small.t[:, 0:8], func=AF.Exp)
        V("tensor_scalar", [lgs], [lgs], out=lgs[:], in0=lgs[:], scalar1=-1.0, scalar2=None, op0=ALU.mult)
        A("activation", [lgs], [g128], out=g128[:], in_=lgs[:], func=AF.Exp, scale=128.0)
        lup = sb(EM, [128, 2, 512], name="lup")
        for d in range(2):
            S.dma(lup.t[0:64, d, :], w_up[d, :, :], [], [lup])
            S.dma(lup.t[64:128, d, :], a_up[d, :, :], [], [lup])
        gup_t = sb(EM, [128, 512], name="gup")
        S.dma(gup_t[:], g_up[:, :], [], [gup_t])

        for b in range(2):
            with ExitStack() as EA:
                S.barrier()
                xt2 = [sb(EA, [128, D], name="xt%d" % i) for i in range(2)]
                junk = sb(EA, [128, D], name="junk")
                xn = sb(EA, [128, D], name="xn")
                ss = sb(EA, [128, 1], name="ss")
                sd = sb(EA, [128, 1], name="sd")
                rstd = sb(EA, [128, 1], name="rstd")
                for c in range(NCH):
                    j = 2 if c < 2 else b
                    xt = xt2[c % 2]
                    S.dma(xt[:], xs[b, c * 128:(c + 1) * 128, :], [], [xt])
                    rms_rstd(EA, xt, junk, ss, sd, rstd, D, "epsn")
                    V("tensor_scalar", [xt, rstd], [xn], out=xn[:], in0=xt[:], scalar1=rstd.t[:, 0:1], scalar2=None,
                      op0=ALU.mult)
                    for half in range(2):
                        ps = P()
                        for q in range(4):
                            kc = half * 4 + q
                            TR([xn], [ps], out=ps.t[:, q * 128:(q + 1) * 128], in_=xn.t[:, kc * 128:(kc + 1) * 128],
                               identity=ident)
                        for q in range(4):
                            kc = half * 4 + q
                            A("activation", [ps, A1, modT], [nxT.s(c)], out=nxT.t[:, kc, c * 128:(c + 1) * 128],
                              in_=ps.t[:, q * 128:(q + 1) * 128], func=AF.Identity,
                              scale=A1.t[:, kc, j:j + 1], bias=modT.t[:, kc, j:j + 1])

            def proj_fm(wb, evac, ncols=128):
                for tt in range(6):
                    ps = P()
                    rd = [nxT.s(c) for c in range(tt * 3, tt * 3 + 3)]
                    for kc in range(8):
                        MM([wb] + rd, [ps], out=ps.t[:, 0:384], lhsT=wb.t[:, kc, 0:128],
                           rhs=nxT.t[:, kc, tt * 384:(tt + 1) * 384], start=(kc == 0), stop=(kc == 7))
                    evac(ps, tt)

            if b == 0:
                tap('nxT', nxT[:], [128, 8, T], [nxT.s(c_) for c_ in range(NCH)], BF16)
            if stop == 'ret':
                continue
            with ExitStack() as ER:
                S.barrier()
                cosT = sb(ER, [128, T], name="cos")
                sinT = sb(ER, [128, T], name="sin")
                S.dma(cosT[:], cos_d[:, :], [], [cosT])
                S.dma(sinT[:], sin_d[:, :], [], [sinT])
                qT = sb(ER, [128, T], name="qT")
                kT = sb(ER, [128, T], name="kT")
                tmpa = sb(ER, [128, 384], name="tmpa")
                tmpb = sb(ER, [128, 384], name="tmpb")
                vtok = sb(ER, [128, NCH, 128], name="vtok")
                sgt = sb(ER, [128, NCH, 128], name="sgt")
                Sf = sb(ER, [128, NCH, 128], name="Sf")
                Sb_ = sb(ER, [128, NCH, 128], name="Sb")
                UB = sb(ER, [128, NCH, 128], name="UB")
                Dm = sb(ER, [128, 128], name="Dm")
                tB = sb(ER, [128, 128], name="tB")
                qdf = sb(ER, [128, 128], name="qdf")
                qdb = sb(ER, [128, 128], name="qdb")
                kd = sb(ER, [128, 2], name="kd")
                kf_2 = [sb(ER, [128, 128], name="kf%d" % i_) for i_ in range(2)]
                kb_2 = [sb(ER, [128, 128], name="kb%d" % i_) for i_ in range(2)]
                sT_2 = [sb(ER, [128, 128], name="sT%d" % i_) for i_ in range(2)]
                qf_2 = [sb(ER, [128, 128], name="qf%d" % i_) for i_ in range(2)]
                qb_2 = [sb(ER, [128, 128], name="qb%d" % i_) for i_ in range(2)]
                rj_2 = [sb(ER, [128, 128], name="rj%d" % i_) for i_ in range(2)]
                rss_2 = [sb(ER, [128, 1], name="rss%d" % i_) for i_ in range(2)]
                rsd_2 = [sb(ER, [128, 1], name="rsd%d" % i_) for i_ in range(2)]
                rrs_2 = [sb(ER, [128, 1], name="rrs%d" % i_) for i_ in range(2)]
                rtok_2 = [sb(ER, [128, 128], name="rtok%d" % i_) for i_ in range(2)]
                rbf_2 = [sb(ER, [128, 128], BF16, name="rbf%d" % i_) for i_ in range(2)]
                for h in range(4):
                    lf = lgs.t[:, h:h + 1]
                    lb = lgs.t[:, 4 + h:5 + h]
                    A("activation", [lgs], [Dm], out=Dm[:], in_=C("distF"), func=AF.Exp, scale=lf)
                    V("tensor_tensor", [Dm], [Dm], out=Dm[:], in0=Dm[:], in1=C("maskF"), op=ALU.mult)
                    A("activation", [lgs], [tB], out=tB[:], in_=C("distB"), func=AF.Exp, scale=lb)
                    V("tensor_tensor", [tB], [tB], out=tB[:], in0=tB[:], in1=C("maskB"), op=ALU.mult)
                    V("tensor_tensor", [tB, Dm], [Dm], out=Dm[:], in0=Dm[:], in1=tB[:], op=ALU.add)
                    A("activation", [lgs], [qdf], out=qdf[:], in_=C("iota1"), func=AF.Exp, scale=lf)
                    A("activation", [lgs], [qdb], out=qdb[:], in_=C("iotab"), func=AF.Exp, scale=lb)
                    A("activation", [lgs], [kd], out=kd.t[:, 0:1], in_=C("colF"), func=AF.Exp, scale=lf)
                    A("activation", [lgs], [kd], out=kd.t[:, 1:2], in_=C("colB"), func=AF.Exp, scale=lb)
                    chk('r1')
                    for which, dst, c0, p0, scl in (("q", qT, h * 128, h * 128, 1.0),
                                                    ("k", kT, 512 + h * 128, 512 + h * 128, 128.0 ** -0.5)):
                        wa = load_cols(w_in, c0)
                        wp = load_cols(w_perm, p0)
                        for tt in range(6):
                            rd = [nxT.s(c) for c in range(tt * 3, tt * 3 + 3)]
                            pa = P()
                            pb = P()
                            sl = slice(tt * 384, (tt + 1) * 384)
                            for kc in range(8):
                                MM([wa] + rd, [pa], out=pa.t[:, 0:384], lhsT=wa.t[:, kc, :], rhs=nxT.t[:, kc, sl],
                                   start=(kc == 0), stop=(kc == 7))
                            for kc in range(8):
                                MM([wp] + rd, [pb], out=pb.t[:, 0:384], lhsT=wp.t[:, kc, :], rhs=nxT.t[:, kc, sl],
                                   start=(kc == 0), stop=(kc == 7))
                            V("scalar_tensor_tensor", [pa, cosT], [tmpa], out=tmpa[:], in0=pa.t[:, 0:384], scalar=scl,
                              in1=cosT.t[:, sl], op0=ALU.mult, op1=ALU.mult)
                            V("scalar_tensor_tensor", [pb, sinT], [tmpb], out=tmpb[:], in0=pb.t[:, 0:384], scalar=scl,
                              in1=sinT.t[:, sl], op0=ALU.mult, op1=ALU.mult)
                            G("tensor_tensor", [tmpa, tmpb], [dst.s(tt)], out=dst.t[:, sl], in0=tmpa[:], in1=tmpb[:],
                              op=ALU.add)
                    chk('r2')
                    wv = load_cols(w_in, 1024 + h * 128)
                    wg = load_cols(w_in, 1536 + h * 128)
                    chk('r2a')
                    for c in range(NCH):
                        if c == 1:
                            chk('r2c')
                        ps = P()
                        for kc in range(8):
                            MM([wv, nxT.s(c)], [ps], out=ps.t[:, 0:128], lhsT=nxT.t[:, kc, c * 128:(c + 1) * 128],
                               rhs=wv.t[:, kc, :], start=(kc == 0), stop=(kc == 7))
                        for kc in range(8):
                            MM([wg, nxT.s(c)], [ps], out=ps.t[:, 128:256], lhsT=nxT.t[:, kc, c * 128:(c + 1) * 128],
                               rhs=wg.t[:, kc, :], start=(kc == 0), stop=(kc == 7))
                        _sk = ''
                        if not (_sk == 'v' and c >= 1):
                            V("tensor_copy", [ps], [vtok.s(c)], out=vtok.t[:, c, :], in_=ps.t[:, 0:128])
                        if not (_sk == 'a' and c >= 1):
                            A("activation", [ps], [sgt.s(c)], out=sgt.t[:, c, :], in_=ps.t[:, 128:256], func=AF.Silu)
                    chk('r3')
                    V("memset", [], [Sf.s(0)], ap=Sf.t[:, 0, :], constant=0.0)
                    V("memset", [], [Sb_.s(1)], ap=Sb_.t[:, 1, :], constant=0.0)
                    for c in range(NCH):
                        kf, kb = kf_2[c % 2], kb_2[c % 2]
                        ps = P()
                        TR([kT.s(c // 3)], [ps], out=ps.t[:, 0:128], in_=kT.t[:, c * 128:(c + 1) * 128], identity=ident)
                        V("tensor_scalar", [ps, kd], [kf], out=kf[:], in0=ps.t[:, 0:128], scalar1=kd.t[:, 0:1],
                          scalar2=None, op0=ALU.mult)
                        V("tensor_scalar", [ps, kd], [kb], out=kb[:], in0=ps.t[:, 0:128], scalar1=kd.t[:, 1:2],
                          scalar2=None, op0=ALU.mult)
                        pu = P()
                        MM([kf, vtok.s(c)], [pu], out=pu.t[:, 0:128], lhsT=kf[:], rhs=vtok.t[:, c, :], start=True, stop=True)
                        MM([kb, vtok.s(c)], [pu], out=pu.t[:, 128:256], lhsT=kb[:], rhs=vtok.t[:, c, :], start=True, stop=True)
                        if c + 1 < NCH:
                            V("scalar_tensor_tensor", [pu, Sf.s(c), g128], [Sf.s(c + 1)], out=Sf.t[:, c + 1, :],
                              in0=Sf.t[:, c, :], scalar=g128.t[:, h:h + 1], in1=pu.t[:, 0:128], op0=ALU.mult, op1=ALU.add)
                        A("activation", [pu], [UB.s(c)], out=UB.t[:, c, :], in_=pu.t[:, 128:256], func=AF.Copy)
                    border = [1, 0] + list(range(17, 1, -1))
                    for a_, b_ in zip(border[:-1], border[1:]):
                        V("scalar_tensor_tensor", [UB.s(a_), Sb_.s(a_), g128], [Sb_.s(b_)], out=Sb_.t[:, b_, :],
                          in0=Sb_.t[:, a_, :], scalar=g128.t[:, 4 + h:5 + h], in1=UB.t[:, a_, :], op0=ALU.mult, op1=ALU.add)
                    if b == 0 and h == 0:
                        tap('qT', qT[:], [128, T], [qT.s(i_) for i_ in range(6)])
                        tap('cos', cosT[:], [128, T], [cosT])
                        tap('sin', sinT[:], [128, T], [sinT])
                        tap('nxT2', nxT[:], [128, 8, T], [nxT.s(c_) for c_ in range(NCH)], BF16)
                        pass
                        tap('kT', kT[:], [128, T], [kT.s(i_) for i_ in range(6)])
                        tap('vtok', vtok[:], [128, NCH, 128], [vtok.s(i_) for i_ in range(NCH)])
                        tap('sgt', sgt[:], [128, NCH, 128], [sgt.s(i_) for i_ in range(NCH)])
                        tap('Sf', Sf[:], [128, NCH, 128], [Sf.s(i_) for i_ in range(NCH)])
                        tap('Sb', Sb_[:], [128, NCH, 128], [Sb_.s(i_) for i_ in range(NCH)])
                        tap('Dm', Dm[:], [128, 128], [Dm])
                        tap('qdf', qdf[:], [128, 128], [qdf])
                        tap('qdb', qdb[:], [128, 128], [qdb])
                        tap('kd', kd[:], [128, 2], [kd])
                        tap('lgs', lgs[:], [128, 8], [lgs])
                        tap('g128', g128[:], [128, 8], [g128])
                    chk('r4')
                    for c in range(2, NCH):
                        sT, qf, qb, rj, rss, rsd, rrs, rtok, rbf = (sT_2[c % 2], qf_2[c % 2], qb_2[c % 2], rj_2[c % 2], rss_2[c % 2],
                                                                     rsd_2[c % 2], rrs_2[c % 2], rtok_2[c % 2], rbf_2[c % 2])
                        sl = slice(c * 128, (c + 1) * 128)
                        ps = P()
                        MM([kT.s(c // 3), qT.s(c // 3)], [ps], out=ps.t[:, 0:128], lhsT=kT.t[:, sl], rhs=qT.t[:, sl],
                           start=True, stop=True)
                        V("tensor_tensor", [ps, Dm], [sT], out=sT[:], in0=ps.t[:, 0:128], in1=Dm[:], op=ALU.mult)
                        G("tensor_tensor", [qT.s(c // 3), qdf], [qf], out=qf[:], in0=qT.t[:, sl], in1=qdf[:], op=ALU.mult)
                        G("tensor_tensor", [qT.s(c // 3), qdb], [qb], out=qb[:], in0=qT.t[:, sl], in1=qdb[:], op=ALU.mult)
                        po = P()
                        MM([sT, vtok.s(c)], [po], out=po.t[:, 0:128], lhsT=sT[:], rhs=vtok.t[:, c, :], start=True, stop=False)
                        MM([qf, Sf.s(c)], [po], out=po.t[:, 0:128], lhsT=qf[:], rhs=Sf.t[:, c, :], start=False, stop=False)
                        MM([qb, Sb_.s(c)], [po], out=po.t[:, 0:128], lhsT=qb[:], rhs=Sb_.t[:, c, :], start=False, stop=True)
                        A("activation", [po], [rj], out=rj[:], in_=po.t[:, 0:128], func=AF.Square)
                        V("tensor_reduce", [rj], [rss], out=rss[:], in_=rj[:], axis=AX.X, op=ALU.add)
                        A("activation", [rss], [rsd], out=rsd[:], in_=rss[:], func=AF.Sqrt, scale=1.0 / 128, bias=C("epsn"))
                        V("reciprocal", [rsd], [rrs], out=rrs[:], in_=rsd[:])
                        V("scalar_tensor_tensor", [po, rrs, sgt.s(c)], [rtok], out=rtok[:], in0=po.t[:, 0:128],
                          scalar=rrs.t[:, 0:1], in1=sgt.t[:, c, :], op0=ALU.mult, op1=ALU.mult)
                        pt = P()
                        TR([rtok], [pt], out=pt.t[:, 0:128], in_=rtok[:], identity=ident)
                        A("activation", [pt], [rbf], out=rbf[:], in_=pt.t[:, 0:128], func=AF.Copy)
                        S.dma(mixs[b, h, :, (c - 2) * 128:(c - 1) * 128], rbf[:], [rbf], [mix_dep[b][c - 2]])

            if stop == 'rwkv':
                continue
            with ExitStack() as EW:
                S.barrier()
                LT = sb(EW, [128, T], name="LT")
                sgG = sb(EW, [128, T], name="sgG")
                rT = sb(EW, [128, T], name="rT")
                kT = sb(EW, [128, T], name="kTw")
                kkT = sb(EW, [128, T], name="kkT")
                vtok = sb(EW, [128, NCH, 128], name="vtokw")
                ytot = sb(EW, [128, 16, 128], name="ytot")
                bsum = sb(EW, [128, NCH, 2], name="bsum")
                st2_2 = [sb(EW, [128, 8], name="st2_%d" % i_) for i_ in range(2)]
                rbf2_2 = [sb(EW, [128, 128], BF16, name="rbf2_%d" % i_) for i_ in range(2)]
                yn_2 = [sb(EW, [128, 128], name="yn%d" % i_) for i_ in range(2)]
                sqy_2 = [sb(EW, [128, 128], name="sqy%d" % i_) for i_ in range(2)]

                def shift_proj(g, c0, dst, raw):
                    wb = load_cols(w_in, c0)

                    def ev(ps, tt):
                        A("activation", [ps], [raw.s(tt)], out=raw.t[:, tt * 384:(tt + 1) * 384], in_=ps.t[:, 0:384],
                          func=AF.Copy)
                    proj_fm(wb, ev)
                    allraw = [raw.s(tt) for tt in range(6)]
                    for (s0, s1) in ((0, 256), (256, T)):
                        V("tensor_scalar", allraw + [mu0], [dst], out=dst.t[:, s0:s1], in0=raw.t[:, s0:s1],
                          scalar1=mu0.t[:, g:g + 1], scalar2=None, op0=ALU.mult)
                        V("scalar_tensor_tensor", allraw + [mu_t, dst], [dst], out=dst.t[:, s0 + 1:s1],
                          in0=raw.t[:, s0:s1 - 1], scalar=mu_t.t[:, g, 0:1], in1=dst.t[:, s0 + 1:s1],
                          op0=ALU.mult, op1=ALU.add)
                        V("scalar_tensor_tensor", allraw + [mu_t, dst], [dst], out=dst.t[:, s0:s1 - 1],
                          in0=raw.t[:, s0 + 1:s1], scalar=mu_t.t[:, g, 1:2], in1=dst.t[:, s0:s1 - 1],
                          op0=ALU.mult, op1=ALU.add)

                with ExitStack() as EP:
                    raw = sb(EP, [128, T], name="raw")
                    shift_proj(12, 2048 + 1536, LT, raw)
                    A("activation", [LT], [LT], out=LT.t[0:64, :], in_=LT.t[0:64, :], func=AF.Tanh)
                    shift_proj(13, 2048 + 1664, sgG, raw)
                    A("activation", [sgG], [sgG], out=sgG[:], in_=sgG[:], func=AF.Sigmoid)
                for hp in range(4):
                    kcol = small.t[:, 8 + hp:9 + hp]
                    kacol = small.t[:, 12 + hp:13 + hp]
                    rkcol = small.t[:, 16 + hp:17 + hp]
                    with ExitStack() as EP:
                        S.barrier()
                        raw = sb(EP, [128, T], name="raw")
                        vT = sb(EP, [128, T], name="vT")
                        sq = sb(EP, [128, 384], name="sq")
                        sq2 = sb(EP, [128, 384], name="sq2")
                        shift_proj(hp, 2048 + hp * 128, rT, raw)
                        shift_proj(4 + hp, 2048 + 512 + hp * 128, kT, raw)
                        shift_proj(8 + hp, 2048 + 1024 + hp * 128, vT, raw)
                        for c in range(NCH):
                            ps = P()
                            TR([vT], [ps], out=ps.t[:, 0:128], in_=vT.t[:, c * 128:(c + 1) * 128], identity=ident)
                            A("activation", [ps], [vtok.s(c)], out=vtok.t[:, c, :], in_=ps.t[:, 0:128], func=AF.Copy)
                        V("tensor_scalar", [kT, small], [kkT], out=kkT[:], in0=kT[:], scalar1=kcol, scalar2=None, op0=ALU.mult)
                        for tt in range(6):
                            sl = slice(tt * 384, (tt + 1) * 384)
                            G("tensor_tensor", [kkT], [sq], out=sq[:], in0=kkT.t[:, sl], in1=kkT.t[:, sl], op=ALU.mult)
                            ps = P()
                            MM([sq], [ps], out=ps.t[:, 0:384], lhsT=C("bones"), rhs=sq[:], start=True, stop=True)
                            A("activation", [ps], [sq2], out=sq2[:], in_=ps.t[:, 0:384], func=AF.Sqrt, bias=C("epsk"), scale=1.0)
                            V("reciprocal", [sq2], [sq2], out=sq2[:], in_=sq2[:])
                            V("tensor_tensor", [kkT, sq2], [kkT], out=kkT.t[:, sl], in0=kkT.t[:, sl], in1=sq2[:], op=ALU.mult)
                        for c in range(2, NCH):
                            G("memset", [], [ytot.s(c)], ap=ytot.t[:, c - 2, :], constant=0.0)

                    with ExitStack() as ED:
                        S.barrier()
                        if b == 0 and hp == 0:
                            r_ = nc.sbuf_bytes_remaining
                            print('SBUF remaining before ED:', r_() if callable(r_) else r_)
                        XD = []
                        for d in range(2):
                            X_ = {}
                            for n_ in ("sg", "aa", "pin", "cx", "cy", "Ek", "Ei", "Er", "Ec", "be", "kka", "kt", "bti", "kti",
                                       "Bm", "Kh", "prod", "GG0", "GG1", "UUn", "PhiT", "tmpP", "Y1T", "Y0s", "sgt_", "Bte0", "Bte1",
                                       "Of_0", "OfT_0",
                                       "Of_1", "OfT_1",
                                       "Zt_0", "Zt_1"):
                                if (d == 0 and n_ == "cy") or (d == 1 and n_ in ("Er", "prod")):
                                    continue
                                X_[n_] = sb(ED, [128, 128], name=n_ + "d%d" % d)
                            X_["KR"] = sb(ED, [128, 256], name="KR%d" % d)
                            for e_ in range(2):
                                for n_ in ("PP0", "PP1", "TX0", "TX1", "VW"):
                                    X_["%s_%d" % (n_, e_)] = sb(ED, [128, 256], name="%s_%d_%d" % (n_, e_, d))
                            X_["AR"] = [sb(ED, [128, 256], name="AR%d_%d" % (e, d)) for e in range(2)]
                            X_["NR"] = [sb(ED, [128, 256], name="NR%d_%d" % (e, d)) for e in range(2)]
                            X_["TK"] = sb(ED, [128, 384], name="TK%d" % d)
                            X_["Hs"] = [sb(ED, [128, 64], name="H%d_%d" % (i, d)) for i in range(2)]
                            X_["Psi"] = sb(ED, [128, 64], name="Psi%d" % d)
                            X_["cols"] = sb(ED, [128, 8], name="cols%d" % d)
                            V("memset", [], [X_["GG0"]], ap=X_["GG0"][:], constant=0.0)
                            V("memset", [], [X_["GG1"]], ap=X_["GG1"][:], constant=0.0)
                            XD.append(X_)

                        def dir_chain(d, X_):
                            order = list(range(NCH)) if d == 0 else [1, 0] + list(range(17, 1, -1))
                            sfx = "f" if d == 0 else "b"
                            mk1, mk2, mk3 = C("mk1" + sfx), C("mk2" + sfx), C("mk3" + sfx)
                            w0c = small.t[:, 20 + hp * 2 + d:21 + hp * 2 + d]
                            a0c = small.t[:, 28 + hp * 2 + d:29 + hp * 2 + d]
                            KR, AR, NR, TK, Hs, Psi, cols = X_["KR"], X_["AR"], X_["NR"], X_["TK"], X_["Hs"], X_["Psi"], X_["cols"]
                            hi = 0
                            V("memset", [], [Hs[0]], ap=Hs[0][:], constant=0.0)
                            for c in order:
                                sl = slice(c * 128, (c + 1) * 128)
                                ps = P()
                                MM([lup, LT], [ps], out=ps.t[:, 0:128], lhsT=lup.t[0:64, d, hp * 128:(hp + 1) * 128],
                                   rhs=LT.t[0:64, sl], start=True, stop=True)
                                ps2 = P()
                                MM([lup, LT], [ps2], out=ps2.t[:, 128:256], lhsT=lup.t[64:128, d, hp * 128:(hp + 1) * 128],
                                   rhs=LT.t[64:128, sl], start=True, stop=True)
                                sg, aa, pin, cx, cy = X_["sg"], X_["aa"], X_["pin"], X_["cx"], X_.get("cy")
                                A("activation", [ps, small], [sg], out=sg[:], in_=ps.t[:, 0:128], func=AF.Sigmoid, bias=w0c, scale=1.0)
                                A("activation", [ps2, small], [aa], out=aa[:], in_=ps2.t[:, 128:256], func=AF.Sigmoid, bias=a0c, scale=1.0)
                                yield
                                pc_ = P()
                                TR([sg], [pc_], out=pc_.t[:, 0:128], in_=sg[:], identity=ident)
                                V("tensor_copy", [pc_], [X_["sgt_"]], out=X_["sgt_"][:], in_=pc_.t[:, 0:128])
                                yield
                                pc2 = P()
                                MM([X_["sgt_"]], [pc2], out=pc2.t[:, 0:128], lhsT=X_["sgt_"][:], rhs=cst.t[:, CST["mk2f"][0] + 128:CST["mk2f"][0] + 256],
                                   start=True, stop=True)
                                V("tensor_copy", [pc2], [pin], out=pin[:], in_=pc2.t[:, 0:128])
                                tot = pin.t[:, 127:128]
                                if d == 0:
                                    V("tensor_tensor", [pin, sg], [cx], out=cx[:], in0=pin[:], in1=sg[:], op=ALU.subtract)
                                    ci, ce = pin, cx
                                else:
                                    V("tensor_scalar", [pin], [cx], out=cx[:], in0=pin[:], scalar1=-1.0, scalar2=tot,
                                      op0=ALU.mult, op1=ALU.add)
                                    V("tensor_tensor", [cx, sg], [cy], out=cy[:], in0=cx[:], in1=sg[:], op=ALU.add)
                                    ci, ce = cy, cx
                                V("tensor_scalar", [pin], [cols], out=cols.t[:, 0:3], in0=C("cv3"), scalar1=tot, scalar2=None, op0=ALU.mult)
                                yield
                                A("activation", [cols], [cols], out=cols.t[:, 3:5], in_=cols.t[:, 1:3], func=AF.Exp)
                                V("tensor_scalar", [cols], [cols], out=cols.t[:, 5:6], in0=cols.t[:, 3:4], scalar1=-1.0, scalar2=None, op0=ALU.mult)
                                Qc, PCc, nQc = cols.t[:, 3:4], cols.t[:, 4:5], cols.t[:, 5:6]
                                Ek, Ei, Er, Ec = X_["Ek"], X_["Ei"], X_.get("Er"), X_["Ec"]
                                A("activation", [ce, cols], [Ek], out=Ek[:], in_=ce[:], func=AF.Exp, scale=-SW, bias=cols.t[:, 0:1])
                                A("activation", [ci, cols], [Ei], out=Ei[:], in_=ci[:], func=AF.Exp, scale=SW, bias=cols.t[:, 1:2])
                                if d == 0:
                                    A("activation", [ci, cols], [Er], out=Er[:], in_=ci[:], func=AF.Exp, scale=-SW, bias=cols.t[:, 0:1])
                                    Eru = Er
                                else:
                                    Eru = Ek
                                A("activation", [ci, cols], [Ec], out=Ec[:], in_=ci[:], func=AF.Exp, scale=SW, bias=cols.t[:, 2:3])
                                be, kka, kt = X_["be"], X_["kka"], X_["kt"]
                                V("tensor_tensor", [aa, kkT], [be], out=be[:], in0=aa[:], in1=kkT.t[:, sl], op=ALU.mult)
                                V("tensor_scalar", [kT, small], [kka], out=kka[:], in0=kT.t[:, sl], scalar1=kacol, scalar2=None, op0=ALU.mult)
                                V("scalar_tensor_tensor", [aa, kka], [kt], out=kt[:], in0=aa[:], scalar=-1.0, in1=kka[:],
                                  op0=ALU.add, op1=ALU.mult)
                                G("tensor_tensor", [kt, kT], [kt], out=kt[:], in0=kt[:], in1=kT.t[:, sl], op=ALU.add)
                                yield
                                G("tensor_tensor", [kkT, Ek], [KR.s(0)], out=KR.t[:, 0:128], in0=kkT.t[:, sl], in1=Ek[:], op=ALU.mult)
                                V("tensor_tensor", [rT, Eru], [KR.s(1)], out=KR.t[:, 128:256], in0=rT.t[:, sl], in1=Eru[:], op=ALU.mult)
                                bti, kti, Bm, Kh = X_["bti"], X_["kti"], X_["Bm"], X_["Kh"]
                                V("tensor_tensor", [be, Ei], [bti], out=bti[:], in0=be[:], in1=Ei[:], op=ALU.mult)
                                G("tensor_tensor", [kt, Ei], [kti], out=kti[:], in0=kt[:], in1=Ei[:], op=ALU.mult)
                                G("tensor_tensor", [be, Ec], [Bm], out=Bm[:], in0=be[:], in1=Ec[:], op=ALU.mult)
                                G("tensor_tensor", [kt, Ec], [Kh], out=Kh[:], in0=kt[:], in1=Ec[:], op=ALU.mult)
                                if d == 0:
                                    prod = X_["prod"]
                                    V("scalar_tensor_tensor", [rT, small, kt], [prod], out=prod[:], in0=rT.t[:, sl], scalar=rkcol,
                                      in1=kt[:], op0=ALU.mult, op1=ALU.mult)
                                    pb_ = P()
                                    MM([prod], [pb_], out=pb_.t[:, 0:2], lhsT=prod[:], rhs=C("bcols"), start=True, stop=True)
                                    A("activation", [pb_], [bsum.s(c)], out=bsum.t[:, c, :], in_=pb_.t[:, 0:2], func=AF.Copy)
                                yield
                                KRd = [KR.s(0), KR.s(1)]
                                pt = P()
                                TR([KR.s(0)], [pt], out=pt.t[:, 0:128], in_=KR.t[:, 0:128], identity=ident)
                                TR([Bm], [pt], out=pt.t[:, 128:256], in_=Bm[:], identity=ident)
                                TR([Kh], [pt], out=pt.t[:, 256:384], in_=Kh[:], identity=ident)
                                A("activation", [pt], [TK], out=TK[:], in_=pt.t[:, 0:384], func=AF.Copy)
                                UUn = X_["UUn"]
                                GG = [X_["GG0"], X_["GG1"]]
                                PhiT, tmpP, Y1T, Y0s = X_["PhiT"], X_["tmpP"], X_["Y1T"], X_["Y0s"]

                                def chain(e):
                                    er = slice(e * 64, (e + 1) * 64)
                                    p1 = P()
                                    MM([bti] + KRd, [p1], out=p1.t[:, 0:256], lhsT=bti.t[er, :], rhs=KR.t[er, :], start=True, stop=True)
                                    MM([bti] + KRd, [p1], out=p1.t[:, 256:384], lhsT=KR.t[er, 0:128], rhs=bti.t[er, :], start=True, stop=True)
                                    yield
                                    p2 = P()
                                    MM([kti] + KRd, [p2], out=p2.t[:, 0:256], lhsT=kti.t[er, :], rhs=KR.t[er, :], start=True, stop=True)
                                    Bt0 = X_["Bte%d" % e]
                                    V("tensor_tensor", [p1], [AR[e]], out=AR[e][:], in0=p1.t[:, 0:256], in1=mk1, op=ALU.mult)
                                    V("tensor_tensor", [p1], [Bt0], out=Bt0[:], in0=p1.t[:, 256:384], in1=mk3, op=ALU.mult)
                                    V("tensor_tensor", [p2], [NR[e]], out=NR[e][:], in0=p2.t[:, 0:256], in1=mk2, op=ALU.mult)
                                    A_ap = AR[e].t[:, 0:128]
                                    PP = [X_["PP0_%d" % e], X_["PP1_%d" % e]]
                                    TX = [X_["TX0_%d" % e], X_["TX1_%d" % e]]
                                    VW = X_["VW_%d" % e]
                                    G("tensor_tensor", [Bt0], [PP[0]], out=PP[0].t[:, 0:128], in0=Bt0[:], in1=C("blk16"), op=ALU.mult)
                                    G("tensor_tensor", [AR[e]], [PP[0]], out=PP[0].t[:, 128:256], in0=A_ap, in1=C("blk16"), op=ALU.mult)
                                    V("tensor_tensor", [PP[0]], [TX[0]], out=TX[0][:], in0=PP[0][:], in1=C("ident2"), op=ALU.add)
                                    pi = 0
                                    ti = 0
                                    for s_ in range(3):
                                        yield
                                        pq = P()
                                        MM([PP[pi]], [pq], out=pq.t[:, 0:128], lhsT=PP[pi].t[:, 128:256], rhs=PP[pi].t[:, 0:128], start=True, stop=True)
                                        MM([PP[pi]], [pq], out=pq.t[:, 128:256], lhsT=PP[pi].t[:, 0:128], rhs=PP[pi].t[:, 128:256], start=True, stop=True)
                                        A("activation", [pq], [PP[1 - pi]], out=PP[1 - pi][:], in_=pq.t[:, 0:256], func=AF.Copy)
                                        pi = 1 - pi
                                        yield
                                        px = P()
                                        MM([PP[pi], TX[ti]], [px], out=px.t[:, 0:128], lhsT=PP[pi].t[:, 128:256], rhs=TX[ti].t[:, 0:128], start=True, stop=True)
                                        MM([PP[pi], TX[ti]], [px], out=px.t[:, 128:256], lhsT=PP[pi].t[:, 0:128], rhs=TX[ti].t[:, 128:256], start=True, stop=True)
                                        V("tensor_tensor", [px, TX[ti]], [TX[1 - ti]], out=TX[1 - ti][:], in0=px.t[:, 0:256], in1=TX[ti][:], op=ALU.add)
                                        ti = 1 - ti
                                    for lvl, on in enumerate(("o16", "o32", "o64")):
                                        lastl = (lvl == 2)
                                        Of, OfT = X_["Of_%d" % e], X_["OfT_%d" % e]
                                        G("tensor_tensor", [Bt0], [Of], out=Of[:], in0=Bt0[:], in1=C(on), op=ALU.mult)
                                        if not lastl:
                                            G("tensor_tensor", [AR[e]], [OfT], out=OfT[:], in0=A_ap, in1=C(on), op=ALU.mult)
                                        yield
                                        pv = P()
                                        MM([Of, TX[ti]], [pv], out=pv.t[:, 128:256], lhsT=Of[:], rhs=TX[ti].t[:, 128:256], start=True, stop=True)
                                        if not lastl:
                                            MM([OfT, TX[ti]], [pv], out=pv.t[:, 0:128], lhsT=OfT[:], rhs=TX[ti].t[:, 0:128], start=True, stop=True)
                                            A("activation", [pv], [VW], out=VW[:], in_=pv.t[:, 0:256], func=AF.Copy)
                                        else:
                                            A("activation", [pv], [VW], out=VW.t[:, 128:256], in_=pv.t[:, 128:256], func=AF.Copy)
                                        yield
                                        px = P()
                                        MM([TX[ti], VW], [px], out=px.t[:, 128:256], lhsT=TX[ti].t[:, 0:128], rhs=VW.t[:, 128:256], start=True, stop=True)
                                        if not lastl:
                                            MM([TX[ti], VW], [px], out=px.t[:, 0:128], lhsT=TX[ti].t[:, 128:256], rhs=VW.t[:, 0:128], start=True, stop=True)
                                            V("tensor_tensor", [px, TX[ti]], [TX[1 - ti]], out=TX[1 - ti][:], in0=px.t[:, 0:256], in1=TX[ti][:], op=ALU.add)
                                        else:
                                            V("tensor_tensor", [px, TX[ti]], [TX[1 - ti]], out=TX[1 - ti].t[:, 128:256], in0=px.t[:, 128:256], in1=TX[ti].t[:, 128:256], op=ALU.add)
                                        ti = 1 - ti
                                    Xc = TX[ti]
                                    Zt = X_["Zt_%d" % e]
                                    G("tensor_copy", [TK], [Zt.s(0)], out=Zt.t[:, 0:64], in_=TK.t[:, e * 64:(e + 1) * 64])
                                    yield
                                    pn = P()
                                    MM([NR[e], vtok.s(c)], [pn], out=pn.t[:, 0:64], lhsT=NR[e].t[:, 0:128], rhs=vtok.t[:, c, er], start=True, stop=True)
                                    A("activation", [pn], [Zt.s(1)], out=Zt.t[:, 64:128], in_=pn.t[:, 0:64], func=AF.Copy)
                                    yield
                                    pg = P()
                                    MM([Xc, Zt.s(0), Zt.s(1)], [pg], out=pg.t[:, 0:128], lhsT=Xc.t[:, 128:256], rhs=Zt[:], start=True, stop=True)
                                    A("activation", [pg], [GG[e]], out=GG[e].t[:, er], in_=pg.t[:, 0:64], func=AF.Copy)
                                    V("tensor_scalar", [pg], [UUn.s(e)], out=UUn.t[:, er], in0=pg.t[:, 64:128], scalar1=-1.0, scalar2=None, op0=ALU.mult)
                                    yield
                                    pY0 = P()
                                    MM([NR[e], vtok.s(c)], [pY0], out=pY0.t[:, er], lhsT=NR[e].t[:, 128:256], rhs=vtok.t[:, c, er], start=True, stop=False)
                                    MM([AR[e], UUn.s(e)], [pY0], out=pY0.t[:, er], lhsT=AR[e].t[:, 128:256], rhs=UUn.t[:, er], start=False, stop=True)
                                    pYP = P()
                                    MM([GG[e], AR[e]], [pYP], out=pYP.t[:, 0:128], lhsT=GG[e][:], rhs=AR[e].t[:, 128:256], start=True, stop=True)
                                    MM([GG[e], TK], [pYP], out=pYP.t[:, 128:256], lhsT=GG[e][:], rhs=TK.t[:, 128:256], start=True, stop=True)
                                    A("activation", [pY0], [Y0s.s(e)], out=Y0s.t[:, er], in_=pY0.t[:, er], func=AF.Copy)
                                    V("tensor_tensor", [KR.s(1), pYP], [Y1T.s(e)], out=Y1T.t[er, :], in0=KR.t[er, 128:256], in1=pYP.t[er, 0:128], op=ALU.subtract)
                                    V("tensor_scalar", [Y1T.s(e), cols], [Y1T.s(e)], out=Y1T.t[er, :], in0=Y1T.t[er, :], scalar1=cols.t[er, 3:4], scalar2=None, op0=ALU.mult)
                                    V("scalar_tensor_tensor", [pYP, cols], [tmpP.s(e)], out=tmpP.t[er, :], in0=pYP.t[er, 128:256], scalar=cols.t[er, 5:6],
                                      in1=cst.t[er, CST["bones"][0]:CST["bones"][0] + 128], op0=ALU.mult, op1=ALU.mult)
                                    V("scalar_tensor_tensor", [tmpP.s(e), cols], [PhiT.s(e)], out=PhiT.t[er, :], in0=cst.t[er, CST["ident"][0]:CST["ident"][0] + 128],
                                      scalar=cols.t[er, 4:5], in1=tmpP.t[er, :], op0=ALU.mult, op1=ALU.add)

                                gens = [chain(0), chain(1)]
                                while gens:
                                    for g_ in list(gens):
                                        try:
                                            next(g_)
                                        except StopIteration:
                                            gens.remove(g_)
                                    yield
                                pPsi = P()
                                MM([TK, vtok.s(c)], [pPsi], out=pPsi.t[:, 0:128], lhsT=TK.t[:, 256:384], rhs=vtok.t[:, c, :], start=True, stop=False)
                                MM([TK, UUn.s(0), UUn.s(1)], [pPsi], out=pPsi.t[:, 0:128], lhsT=TK.t[:, 128:256], rhs=UUn[:], start=False, stop=True)
                                A("activation", [pPsi], [Psi.s(0)], out=Psi.t[0:64, :], in_=pPsi.t[0:64, 0:64], func=AF.Copy)
                                A("activation", [pPsi], [Psi.s(1)], out=Psi.t[64:128, :], in_=pPsi.t[64:128, 64:128], func=AF.Copy)
                                if c >= 2:
                                    pyy2 = [P(), P()]
                                    for e in range(2):
                                        er = slice(e * 64, (e + 1) * 64)
                                        MM([Y1T.s(e), Hs[hi]], [pyy2[e]], out=pyy2[e].t[:, er], lhsT=Y1T.t[er, :], rhs=Hs[hi].t[er, :], start=True, stop=True)
                                    for e in range(2):
                                        er = slice(e * 64, (e + 1) * 64)
                                        V("tensor_tensor", [pyy2[e], Y0s.s(e)], [Y0s.s(e)], out=Y0s.t[:, er], in0=pyy2[e].t[:, er], in1=Y0s.t[:, er], op=ALU.add)
                                    G("tensor_tensor", [Y0s.s(0), Y0s.s(1), ytot.s(c)], [ytot.s(c)], out=ytot.t[:, c - 2, :], in0=ytot.t[:, c - 2, :], in1=Y0s[:], op=ALU.add)
                                pH = P()
                                MM([PhiT.s(0), PhiT.s(1), Hs[hi]], [pH], out=pH.t[:, 0:64], lhsT=PhiT[:], rhs=Hs[hi][:], start=True, stop=True)
                                V("tensor_tensor", [pH, Psi.s(0), Psi.s(1)], [Hs[1 - hi]], out=Hs[1 - hi][:], in0=pH.t[:, 0:64], in1=Psi[:], op=ALU.add)
                                hi = 1 - hi
                                yield

                        dgens = [dir_chain(0, XD[0]), dir_chain(1, XD[1])]
                        while dgens:
                            for g_ in list(dgens):
                                try:
                                    next(g_)
                                except StopIteration:
                                    dgens.remove(g_)

                    S.barrier()
                    for c in range(2, NCH):
                        st2, rbf2, yn, sqy = st2_2[c % 2], rbf2_2[c % 2], yn_2[c % 2], sqy_2[c % 2]
                        sl = slice(c * 128, (c + 1) * 128)
                        yc = ytot.t[:, c - 2, :]
                        G("tensor_tensor", [ytot.s(c)], [sqy], out=sqy[:], in0=yc, in1=yc, op=ALU.mult)
                        for e in range(2):
                            er = slice(e * 64, (e + 1) * 64)
                            V("tensor_reduce", [ytot.s(c)], [st2], out=st2.t[:, e:e + 1], in_=ytot.t[:, c - 2, er], axis=AX.X, op=ALU.add)
                            V("tensor_reduce", [sqy], [st2], out=st2.t[:, 2 + e:3 + e], in_=sqy.t[:, er], axis=AX.X, op=ALU.add)
                        V("tensor_scalar", [st2], [st2], out=st2.t[:, 0:2], in0=st2.t[:, 0:2], scalar1=1.0 / 64, scalar2=None, op0=ALU.mult)
                        V("tensor_tensor", [st2], [st2], out=st2.t[:, 4:6], in0=st2.t[:, 0:2], in1=st2.t[:, 0:2], op=ALU.mult)
                        V("scalar_tensor_tensor", [st2], [st2], out=st2.t[:, 6:8], in0=st2.t[:, 2:4], scalar=1.0 / 64, in1=st2.t[:, 4:6],
                          op0=ALU.mult, op1=ALU.subtract)
                        A("activation", [st2], [st2], out=st2.t[:, 2:4], in_=st2.t[:, 6:8], func=AF.Sqrt, bias=C("epsg"), scale=1.0)
                        V("reciprocal", [st2], [st2], out=st2.t[:, 4:6], in_=st2.t[:, 2:4])
                        for e in range(2):
                            er = slice(e * 64, (e + 1) * 64)
                            V("tensor_scalar", [ytot.s(c), st2], [yn], out=yn.t[:, er], in0=ytot.t[:, c - 2, er], scalar1=st2.t[:, e:e + 1],
                              scalar2=st2.t[:, 4 + e:5 + e], op0=ALU.subtract, op1=ALU.mult)
                        G("tensor_tensor", [yn, lnw_t], [yn], out=yn[:], in0=yn[:], in1=lnw_t.t[:, hp * 128:(hp + 1) * 128], op=ALU.mult)
                        G("tensor_tensor", [yn, lnb_t], [yn], out=yn[:], in0=yn[:], in1=lnb_t.t[:, hp * 128:(hp + 1) * 128], op=ALU.add)
                        for e in range(2):
                            er = slice(e * 64, (e + 1) * 64)
                            V("scalar_tensor_tensor", [vtok.s(c), bsum.s(c), yn], [yn], out=yn.t[:, er], in0=vtok.t[:, c, er],
                              scalar=bsum.t[:, c, e:e + 1], in1=yn.t[:, er], op0=ALU.mult, op1=ALU.add)
                        pg = P()
                        MM([sgG, gup_t], [pg], out=pg.t[:, 0:128], lhsT=sgG.t[:, sl], rhs=gup_t.t[:, hp * 128:(hp + 1) * 128], start=True, stop=True)
                        V("tensor_tensor", [pg, yn], [yn], out=yn[:], in0=pg.t[:, 0:128], in1=yn[:], op=ALU.mult)
                        pt = P()
                        TR([yn], [pt], out=pt.t[:, 0:128], in_=yn[:], identity=ident)
                        A("activation", [pt], [rbf2], out=rbf2[:], in_=pt.t[:, 0:128], func=AF.Copy)
                        S.dma(mixs[b, 4 + hp, :, (c - 2) * 128:(c - 1) * 128], rbf2[:], [rbf2], [mix_dep[b][c - 2]])

            with ExitStack() as EO:
                S.barrier()
                g1bc = sb(EO, [128, D], name="g1bc")
                bcast_row(EO, 16, b, g1bc)
                wo = sb(EO, [128, 8, D], BF16, name="wo")
                stg = sb(EO, [128, D], name="stg")
                for m in range(8):
                    S.dma(stg[:], w_out[m * 128:(m + 1) * 128, :], [], [stg])
                    G("tensor_copy", [stg], [wo], out=wo.t[:, m, :], in_=stg[:])
                mt = sb(EO, [128, 8, 128], BF16, name="mt")
                xt = sb(EO, [128, D], name="xto")
                ht = sb(EO, [128, D], name="hto")
                for c in range(16):
                    S.dma(mt[:], mixs[b].rearrange("m p t -> p m t")[:, :, c * 128:(c + 1) * 128], [mix_dep[b][c]], [mt])
                    S.dma(xt[:], xs[b, 256 + c * 128:256 + (c + 1) * 128, :], [], [xt])
                    for half in range(2):
                        hs = slice(half * 512, (half + 1) * 512)
                        ps = P()
                        for m in range(8):
                            MM([mt, wo], [ps], out=ps.t[:, :], lhsT=mt.t[:, m, :], rhs=wo.t[:, m, hs], start=(m == 0), stop=(m == 7))
                        V("tensor_tensor", [ps, g1bc], [ht.s(half)], out=ht.t[:, hs], in0=ps.t[:, :], in1=g1bc.t[:, hs], op=ALU.mult)
                        G("tensor_tensor", [ht.s(half), xt], [ht.s(half)], out=ht.t[:, hs], in0=ht.t[:, hs], in1=xt.t[:, hs], op=ALU.add)
                    S.dma(out[b, c * 128:(c + 1) * 128, :], ht[:], [ht.s(0), ht.s(1)], [out_dep[b][c]])

    EG.close()
    if "ffn" in phases:
      with ExitStack() as EF:
        S.barrier()
        w1b = sb(EF, [128, 8, 4096], BF16, name="w1b")
        w2b = sb(EF, [128, 32, D], BF16, name="w2b")
        stg = [sb(EF, [128, D], name="stgf%d" % i) for i in range(2)]
        k_ = 0
        for kc in range(8):
            for q in range(4):
                st = stg[k_ % 2]
                k_ += 1
                S.dma(st[:], w_ff1[kc * 128:(kc + 1) * 128, q * 1024:(q + 1) * 1024], [], [st])
                G("tensor_copy", [st], [w1b], out=w1b.t[:, kc, q * 1024:(q + 1) * 1024], in_=st[:])
        for fc in range(32):
            st = stg[k_ % 2]
            k_ += 1
            S.dma(st[:], w_ff2[fc * 128:(fc + 1) * 128, :], [], [st])
            G("tensor_copy", [st], [w2b], out=w2b.t[:, fc, :], in_=st[:])
        b1t = sb(EF, [128, 32], name="b1t")
        b2t = sb(EF, [128, D], name="b2t")
        fgt = sb(EF, [128, D], name="fgt")
        g2bc = sb(EF, [128, D], name="g2bc")
        S.dma(b1t[:], b1T[:, :], [], [b1t])
        S.dma(b2t[:], b2bc[:, :], [], [b2t])
        S.dma(fgt[:], fgbc[:, :], [], [fgt])
        n2T = sb(EF, [128, 8, 256], BF16, name="n2T")
        hts = [sb(EF, [128, D], name="hts%d" % i) for i in range(2)]
        wk1 = sb(EF, [128, D], name="wk1")
        wk2 = sb(EF, [128, D], name="wk2")
        rl = [sb(EF, [128, 256], name="rl%d" % i) for i in range(3)]
        h1 = [sb(EF, [128, 256], BF16, name="h1_%d" % i) for i in range(3)]
        ss = sb(EF, [128, 1], name="ssf")
        sd = sb(EF, [128, 1], name="sdf")
        rstd = sb(EF, [128, 1], name="rstdf")
        fidx = [0]

        def P4():
            p = PS[fidx[0] % 4]
            fidx[0] += 1
            return p

        for b in range(2):
            bcast_row(EF, 40, b, g2bc)
            for tt in range(8):
                for s2 in range(2):
                    c = tt * 2 + s2
                    ht = hts[s2]
                    S.dma(ht[:], out[b, c * 128:(c + 1) * 128, :], [out_dep[b][c]], [ht])
                    rms_rstd(EF, ht, wk2, ss, sd, rstd, D, "epsn")
                    V("tensor_scalar", [ht, rstd], [wk1], out=wk1[:], in0=ht[:], scalar1=rstd.t[:, 0:1], scalar2=None, op0=ALU.mult)
                    for half in range(2):
                        ps = P4()
                        for q in range(4):
                            kc = half * 4 + q
                            TR([wk1], [ps], out=ps.t[:, q * 128:(q + 1) * 128], in_=wk1.t[:, kc * 128:(kc + 1) * 128], identity=ident)
                        for q in range(4):
                            kc = half * 4 + q
                            A("activation", [ps, A2, modT], [n2T], out=n2T.t[:, kc, s2 * 128:(s2 + 1) * 128],
                              in_=ps.t[:, q * 128:(q + 1) * 128], func=AF.Identity, scale=A2.t[:, kc, b:b + 1],
                              bias=modT.t[:, 24 + kc, b:b + 1])
                acc = [PS[4], PS[5], PS[6], PS[7]]
                def ffn1(fc):
                    ps = P4()
                    for kc in range(8):
                        MM([w1b, n2T], [ps], out=ps.t[:, 0:256], lhsT=w1b.t[:, kc, fc * 128:(fc + 1) * 128], rhs=n2T.t[:, kc, :],
                           start=(kc == 0), stop=(kc == 7))
                    r_ = rl[fc % 3]
                    h_ = h1[fc % 3]
                    A("activation", [ps, b1t], [r_], out=r_[:], in_=ps.t[:, 0:256], func=AF.Relu, bias=b1t.t[:, fc:fc + 1], scale=1.0)
                    G("tensor_tensor", [r_], [h_], out=h_[:], in0=r_[:], in1=r_[:], op=ALU.mult)

                ffn1(0)
                for fc in range(32):
                    if fc + 1 < 32:
                        ffn1(fc + 1)
                    h_ = h1[fc % 3]
                    for s2 in range(2):
                        for half in range(2):
                            MM([h_, w2b], [acc[s2 * 2 + half]], out=acc[s2 * 2 + half].t[:, :], lhsT=h_.t[:, s2 * 128:(s2 + 1) * 128],
                               rhs=w2b.t[:, fc, half * 512:(half + 1) * 512], start=(fc == 0), stop=(fc == 31))
                for s2 in range(2):
                    c = tt * 2 + s2
                    for half in range(2):
                        hs = slice(half * 512, (half + 1) * 512)
                        V("tensor_tensor", [acc[s2 * 2 + half], b2t], [wk1], out=wk1.t[:, hs], in0=acc[s2 * 2 + half].t[:, :], in1=b2t.t[:, hs], op=ALU.add)
                    G("tensor_tensor", [wk1, g2bc], [wk1], out=wk1[:], in0=wk1[:], in1=g2bc[:], op=ALU.mult)
                    G("tensor_tensor", [wk1, hts[s2]], [wk1], out=wk1[:], in0=wk1[:], in1=hts[s2][:], op=ALU.add)
                    rms_rstd(EF, wk1, wk2, ss, sd, rstd, D, "epsn")
                    V("scalar_tensor_tensor", [wk1, rstd, fgt], [wk2], out=wk2[:], in0=wk1[:], scalar=rstd.t[:, 0:1], in1=fgt[:],
                      op0=ALU.mult, op1=ALU.mult)
                    S.dma(out[b, c * 128:(c + 1) * 128, :], wk2[:], [wk2], [out_dep[b][c]])
    S.finish()
    ES.close()
    nc._nins = S.nins
    return nc


_NC = None
LAST = None
NCORE = 8
BUILD_KW = {}


def kernel(**inp):
    global _NC
    f = lambda a: np.ascontiguousarray(np.asarray(a, dtype=np.float32))
    x, c, ctx, c_ctx = f(inp["x"]), f(inp["c"]), f(inp["ctx"]), f(inp["c_ctx"])
    w_in = f(inp["w_in"][0])
    perm = _partner_perm()
    qp = np.concatenate([w_in[:, h * 128 + perm] for h in range(4)], 1)
    kp = np.concatenate([w_in[:, 512 + h * 128 + perm] for h in range(4)], 1)
    cosT, sinT = _rope_tables()

    def colmajor(v, n):
        return f(np.asarray(v).reshape(n, 128).T)

    def bc(v):
        return f(np.broadcast_to(np.asarray(v).reshape(1, -1), (128, np.asarray(v).size)))

    shared = {
        "w_ada": f(inp["w_ada"][0]), "b_adaT": colmajor(inp["b_ada"][0], 48),
        "n1g": colmajor(inp["norm1_g"][0], 8), "n2g": colmajor(inp["norm2_g"][0], 8),
        "w_in": w_in, "w_perm": f(np.concatenate([qp, kp], 1)),
        "ldec": bc(inp["ret_log_decay"][0].reshape(-1)),
        "mu": f(np.asarray(inp["rwkv_shift_mu"][0]).reshape(2, 14, 128).transpose(2, 1, 0)),
        "w0T": f(np.asarray(inp["rwkv_w0"][0]).reshape(2, 4, 128).transpose(2, 1, 0)),
        "a0T": f(np.asarray(inp["rwkv_a0"][0]).reshape(2, 4, 128).transpose(2, 1, 0)),
        "w_up": f(inp["rwkv_w_up"][0]), "a_up": f(inp["rwkv_a_up"][0]), "g_up": f(inp["rwkv_g_up"][0]),
        "k_kT": colmajor(inp["rwkv_k_k"][0], 4), "k_aT": colmajor(inp["rwkv_k_a"][0], 4),
        "r_kT": colmajor(inp["rwkv_r_k"][0], 4),
        "lnw": bc(inp["rwkv_ln_w"][0]), "lnb": bc(inp["rwkv_ln_b"][0]),
        "w_out": f(inp["w_out"][0]), "w_ff1": f(inp["w_ff1"][0]), "b1T": colmajor(inp["b_ff1"][0], 32),
        "w_ff2": f(inp["w_ff2"][0]), "b2bc": bc(inp["b_ff2"][0]), "fgbc": bc(inp["final_g"]),
        "cst": _consts(), "cosT": cosT, "sinT": sinT,
    }
    in_maps = []
    ncore = NCORE
    for i in range(ncore):
        b0, b1 = 2 * i, 2 * i + 1
        xs = np.stack([np.concatenate([ctx[b0], x[b0]], 0), np.concatenate([ctx[b1], x[b1]], 0)], 0)
        cv = np.stack([c[b0], c[b1], c_ctx], 0)
        cTt = f(cv.reshape(3, 8, 128).transpose(2, 1, 0))
        m = dict(shared)
        m["xs"] = f(xs)
        m["cT"] = cTt
        in_maps.append(m)
    if _NC is None:
        try:
            _NC = build(**BUILD_KW)
        except _Stop as e_:
            _NC = e_.args[0]
    res = run_bass_kernel_spmd(_NC, in_maps, core_ids=list(range(ncore)))
    global LAST
    LAST = res.results
    outs = [np.asarray(r["out"]) for r in res.results]
    full = np.concatenate(outs, 0).astype(np.float32)
    return full
```

```python
import math
import numpy as np
from contextlib import ExitStack
import concourse.bass as bass
import concourse.mybir as mybir
from concourse.bass_utils import run_bass_kernel_spmd

F32 = mybir.dt.float32
BF16 = mybir.dt.bfloat16
AF = mybir.ActivationFunctionType
ALU = mybir.AluOpType
AX = mybir.AxisListType

T = 2304
NCH = 18
D = 1024
SW = math.exp(-0.5)
NSLOT = 24


class _Stop(Exception):
    pass


class Dep:
    __slots__ = ("w", "r", "excl")

    def __init__(self):
        self.w = None
        self.r = {}
        self.excl = False


class Tl:
    def __init__(self, t):
        self.t = t
        self.d = Dep()
        self.k = {}

    def __getitem__(self, i):
        return self.t[i]

    def s(self, key):
        if key not in self.k:
            self.k[key] = Dep()
        return self.k[key]


def _deps(lst):
    out = []
    for x in lst:
        if isinstance(x, Tl):
            out.append(x.d)
        elif isinstance(x, Dep):
            out.append(x)
        elif isinstance(x, (list, tuple)):
            out.extend(_deps(x))
        elif x is None:
            pass
        else:
            raise TypeError(type(x))
    return out


class Sched:
    def __init__(self, nc, ES):
        self.nc = nc
        self.E = {}
        self.sems = {}
        for nm, obj in (("pe", nc.tensor), ("act", nc.scalar), ("dve", nc.vector),
                        ("pool", nc.gpsimd), ("sp", nc.sync)):
            sem = ES.enter_context(nc.semaphore("s_" + nm))
            self.E[nm] = dict(o=obj, sem=sem, cnt=0, waited={})
            self.sems[nm] = sem
        self.slots = []
        for i in range(NSLOT):
            sem = ES.enter_context(nc.semaphore("dq%d" % i))
            self.sems["dq%d" % i] = sem
            self.slots.append(["dq%d" % i, 0])
        self.rr = 0
        self.nins = 0

    def _wait(self, eng, needs):
        e = self.E[eng]
        for k, c in needs.items():
            if e["waited"].get(k, 0) < c:
                e["o"].wait_ge(self.sems[k], c)
                e["waited"][k] = c

    def _needs(self, eng, R, W):
        needs = {}

        def add(tok, raw):
            if tok is None:
                return
            k, c = tok
            if k == eng and eng == "pe":
                return
            if needs.get(k, 0) < c:
                needs[k] = c

        for d in R:
            add(d.w, True)
            if d.excl:
                for k, c in d.r.items():
                    add((k, c), False)
        for d in W:
            add(d.w, False)
            for k, c in d.r.items():
                add((k, c), False)
        return needs

    def _commit(self, tok, R, W):
        k, c = tok
        for d in R:
            if d.r.get(k, 0) < c:
                d.r[k] = c
        for d in W:
            d.w = tok
            d.r = {}

    def op(self, eng, fn, R, W, **kw):
        R = _deps(R)
        W = _deps(W)
        self._wait(eng, self._needs(eng, R, W))
        e = self.E[eng]
        ins = getattr(e["o"], fn)(**kw)
        e["cnt"] += 1
        ins.then_inc(e["sem"], 1)
        self._commit((eng, e["cnt"]), R, W)
        self.nins += 1

    def dma(self, out, in_, R, W, q="sp"):
        R = _deps(R)
        W = _deps(W)
        needs = self._needs(q, R, W)
        slot = self.slots[self.rr]
        self.rr = (self.rr + 1) % len(self.slots)
        if slot[1] > 0:
            needs[slot[0]] = max(needs.get(slot[0], 0), slot[1] * 16)
        self._wait(q, needs)
        self.E[q]["o"].dma_start(out=out, in_=in_).then_inc(self.sems[slot[0]], 16)
        slot[1] += 1
        self._commit((slot[0], slot[1] * 16), R, W)
        self.nins += 1

    def barrier(self):
        needs = {nm: e["cnt"] for nm, e in self.E.items() if e["cnt"] > 0}
        for sl in self.slots:
            if sl[1] > 0:
                needs[sl[0]] = sl[1] * 16
        for nm in self.E:
            n2 = {k: v for k, v in needs.items() if k != nm}
            self._wait(nm, n2)

    def finish(self):
        needs = {s[0]: s[1] * 16 for s in self.slots if s[1] > 0}
        self._wait("sp", needs)


CST = {}
_off = 0
for _n, _w in (("ident", 128), ("bones", 128), ("distF", 128), ("distB", 128), ("maskF", 128), ("maskB", 128),
               ("iota1", 128), ("iotab", 128), ("colF", 1), ("colB", 1), ("bcols", 2), ("ones", 128),
               ("mk1f", 256), ("mk2f", 256), ("mk3f", 128), ("mk1b", 256), ("mk2b", 256), ("mk3b", 128),
               ("epsn", 1), ("epsg", 1), ("epsk", 1), ("blk16", 128), ("o16", 128), ("o32", 128), ("o64", 128), ("ident2", 256), ("cv3", 3)):
    CST[_n] = (_off, _w)
    _off += _w
NCST = _off


def _consts():
    c = np.zeros((128, NCST), np.float32)

    def put(n, a):
        o, w = CST[n]
        c[:, o:o + w] = np.asarray(a, np.float32).reshape(128, w)

    i = np.arange(128)
    a = i[:, None]
    b = i[None, :]
    put("ident", np.eye(128))
    put("bones", (a // 64) == (b // 64))
    put("distF", np.maximum(b - a, 0))
    put("distB", np.maximum(a - b, 0))
    put("maskF", b >= a)
    put("maskB", a > b)
    put("iota1", np.broadcast_to(b + 1, (128, 128)))
    put("iotab", np.broadcast_to(128 - b, (128, 128)))
    put("colF", 127 - i)
    put("colB", i)
    put("bcols", np.stack([(i < 64), (i >= 64)], 1))
    put("ones", np.ones((128, 128)))
    Us = (a < b).astype(np.float32)
    Ui = (a <= b).astype(np.float32)
    Ls = (a > b).astype(np.float32)
    put("mk1f", np.concatenate([-Us, Ui], 1))
    put("mk2f", np.concatenate([Us, Ui], 1))
    put("mk3f", -Ls)
    put("mk1b", np.concatenate([-Ls, Ls], 1))
    put("mk2b", np.concatenate([Ls, Ls], 1))
    put("mk3b", -Us)
    put("epsn", np.full(128, 1e-6))
    put("epsg", np.full(128, 64e-5))
    put("epsk", np.full(128, 1e-12))
    blk = lambda n: (a // n) == (b // n)
    put("blk16", blk(16))
    put("o16", blk(32) & ~blk(16))
    put("o32", blk(64) & ~blk(32))
    put("o64", ~blk(64))
    put("ident2", np.concatenate([np.eye(128), np.eye(128)], 1))
    put("cv3", np.broadcast_to(np.array([0.5 * SW, -0.5 * SW, -SW]), (128, 3)))
    return c


def _rope_tables():
    half = 64
    inv = np.power(np.float32(10000.0), -np.arange(0, half, 2, dtype=np.float32) / np.float32(half)).astype(np.float32)
    t = np.arange(2048)
    rows = (t // 64).astype(np.float32)
    cols = (t % 64).astype(np.float32)
    cosT = np.ones((128, T), np.float32)
    sinT = np.zeros((128, T), np.float32)
    for p in range(128):
        f = p % 32
        pos = rows if p < 64 else cols
        ang = (pos * inv[f]).astype(np.float32)
        sgn = -1.0 if (p % 64) < 32 else 1.0
        cosT[p, 256:] = np.cos(ang)
        sinT[p, 256:] = sgn * np.sin(ang)
    return cosT, sinT


def _partner_perm():
    p = np.arange(128)
    return np.where((p % 64) < 32, p + 32, p - 32)


def build(phases=("mix", "ffn"), stop=None, taps=()):
    nc = bass.Bass("TRN2", target_bir_lowering=False)
    ES = ExitStack()

    def dram(name, shape, dt=F32, kind="ExternalInput"):
        return nc.dram_tensor(name, list(shape), dt, kind=kind).ap()

    xs = dram("xs", [2, T, D])
    cT = dram("cT", [128, 8, 3])
    w_ada = dram("w_ada", [D, 6144])
    b_adaT = dram("b_adaT", [128, 48])
    n1g = dram("n1g", [128, 8])
    n2g = dram("n2g", [128, 8])
    w_in = dram("w_in", [D, 3840])
    w_perm = dram("w_perm", [D, 1024])
    ldec = dram("ldec", [128, 8])
    mu = dram("mu", [128, 14, 2])
    w0T = dram("w0T", [128, 4, 2])
    a0T = dram("a0T", [128, 4, 2])
    w_up = dram("w_up", [2, 64, 512])
    a_up = dram("a_up", [2, 64, 512])
    g_up = dram("g_up", [128, 512])
    k_kT = dram("k_kT", [128, 4])
    k_aT = dram("k_aT", [128, 4])
    r_kT = dram("r_kT", [128, 4])
    lnw = dram("lnw", [128, 512])
    lnb = dram("lnb", [128, 512])
    w_out = dram("w_out", [D, D])
    w_ff1 = dram("w_ff1", [D, 4096])
    b1T = dram("b1T", [128, 32])
    w_ff2 = dram("w_ff2", [4096, D])
    b2bc = dram("b2bc", [128, D])
    fgbc = dram("fgbc", [128, D])
    cst_d = dram("cst", [128, NCST])
    cos_d = dram("cosT", [128, T])
    sin_d = dram("sinT", [128, T])
    out = dram("out", [2, 2048, D], kind="ExternalOutput")
    mixs = dram("mixs", [2, 8, 128, 2048], BF16, kind="ExternalOutput")
    out_dep = [[Dep() for _ in range(16)] for _ in range(2)]
    mix_dep = [[Dep() for _ in range(16)] for _ in range(2)]

    S = Sched(nc, ES)
    cnt = [0]

    def tap(name, src, shape, R, dt=F32):
        if name in taps:
            dtn = dram('tap_' + name, shape, dt, kind='ExternalOutput')
            S.dma(dtn, src, R, [Dep()])

    def chk(name):
        if stop == name:
            S.finish()
            nc._nins = S.nins
            raise _Stop(nc)

    def sb(ES_, shape, dt=F32, name=None):
        cnt[0] += 1
        return Tl(ES_.enter_context(nc.sbuf_tensor("t%d_%s" % (cnt[0], name or ""), list(shape), dt)))

    PS = [Tl(ES.enter_context(nc.psum_tensor("ps%d" % i, [128, 512], F32))) for i in range(8)]
    for p_ in PS:
        p_.d.excl = True
    pidx = [0]

    def P():
        p = PS[pidx[0]]
        pidx[0] = (pidx[0] + 1) % 8
        return p

    def V(fn, R, W, **kw):
        S.op("dve", fn, R, W, **kw)

    def A(fn, R, W, **kw):
        S.op("act", fn, R, W, **kw)

    def G(fn, R, W, **kw):
        S.op("pool", fn, R, W, **kw)

    def MM(R, W, **kw):
        S.op("pe", "matmul", R, W, **kw)

    def TR(R, W, **kw):
        S.op("pe", "transpose", R, W, **kw)

    cst = sb(ES, [128, NCST], name="cst")
    S.dma(cst[:], cst_d[:, :], [], [cst])

    def C(n):
        o, w = CST[n]
        return cst.t[:, o:o + w]

    ident = C("ident")
    modT = sb(ES, [128, 48, 3], name="modT")
    A1 = sb(ES, [128, 8, 3], name="A1")
    A2 = sb(ES, [128, 8, 3], name="A2")
    EG = ExitStack()
    wst = [sb(EG, [128, 8, 128], name="wst%d" % i) for i in range(2)]
    wsti = [0]
    wbf = [sb(EG, [128, 8, 128], BF16, name="wbf%d" % i) for i in range(3)]
    wbfi = [0]

    def load_cols(src, c0, n=128, cast_eng="pool"):
        st = wst[wsti[0] % 2]
        wsti[0] += 1
        S.dma(st.t[:, :, 0:n], src.rearrange("(kc p) c -> p kc c", p=128)[:, :, c0:c0 + n], [], [st])
        wb = wbf[wbfi[0] % len(wbf)]
        wbfi[0] += 1
        S.op(cast_eng, "tensor_copy", [st], [wb], out=wb.t[:, :, 0:n], in_=st.t[:, :, 0:n])
        return wb

    with ExitStack() as E0:
        ct = sb(E0, [128, 8, 3], name="ct")
        silc = sb(E0, [128, 8, 3], name="silc")
        bad = sb(E0, [128, 48], name="bad")
        g1n = sb(E0, [128, 8], name="g1n")
        g2n = sb(E0, [128, 8], name="g2n")
        S.dma(ct[:], cT[:, :, :], [], [ct])
        S.dma(bad[:], b_adaT[:, :], [], [bad])
        S.dma(g1n[:], n1g[:, :], [], [g1n])
        S.dma(g2n[:], n2g[:, :], [], [g2n])
        A("activation", [ct], [silc], out=silc[:], in_=ct[:], func=AF.Silu)
        for oc in range(48):
            st = wst[wsti[0] % 2]
            wsti[0] += 1
            S.dma(st[:], w_ada.rearrange("(kc p) c -> p kc c", p=128)[:, :, oc * 128:(oc + 1) * 128], [], [st])
            ps = P()
            for kc in range(8):
                MM([st, silc], [ps], out=ps.t[:, 0:3], lhsT=st.t[:, kc, :], rhs=silc.t[:, kc, :],
                   start=(kc == 0), stop=(kc == 7))
            V("tensor_scalar", [ps, bad], [modT], out=modT.t[:, oc, :], in0=ps.t[:, 0:3],
              scalar1=bad.t[:, oc:oc + 1], scalar2=None, op0=ALU.add)
        for kc in range(8):
            V("tensor_scalar", [modT, g1n], [A1], out=A1.t[:, kc, :], in0=modT.t[:, 8 + kc, :], scalar1=1.0,
              scalar2=g1n.t[:, kc:kc + 1], op0=ALU.add, op1=ALU.mult)
            V("tensor_scalar", [modT, g2n], [A2], out=A2.t[:, kc, :], in0=modT.t[:, 32 + kc, :], scalar1=1.0,
              scalar2=g2n.t[:, kc:kc + 1], op0=ALU.add, op1=ALU.mult)

    tap('modT', modT[:], [128, 48, 3], [modT])
    tap('A1', A1[:], [128, 8, 3], [A1])

    def bcast_row(ES_, base, j, dst):
        dg = sb(ES_, [128, 128], name="dg")
        for half in range(2):
            ps = P()
            for q in range(4):
                kc = half * 4 + q
                V("tensor_scalar", [modT], [dg], out=dg[:], in0=ident, scalar1=modT.t[:, base + kc, j:j + 1],
                  scalar2=None, op0=ALU.mult)
                MM([dg], [ps], out=ps.t[:, q * 128:(q + 1) * 128], lhsT=C("ones"), rhs=dg[:], start=True, stop=True)
            A("activation", [ps], [dst], out=dst.t[:, half * 512:(half + 1) * 512], in_=ps.t[:, :], func=AF.Copy)

    def rms_rstd(ES_, xt, junk, ss, sd, rstd, width, eps_name):
        A("activation", [xt], [junk], out=junk.t[:, 0:width], in_=xt.t[:, 0:width], func=AF.Square)
        V("tensor_reduce", [junk], [ss], out=ss[:], in_=junk.t[:, 0:width], axis=AX.X, op=ALU.add)
        A("activation", [ss], [sd], out=sd[:], in_=ss[:], func=AF.Sqrt, scale=1.0 / width, bias=C(eps_name))
        V("reciprocal", [sd], [rstd], out=rstd[:], in_=sd[:])

    if "mix" in phases:
      with ExitStack() as EM:
        nxT = sb(EM, [128, 8, T], BF16, name="nxT")
        lnw_t = sb(EM, [128, 512], name="lnw")
        lnb_t = sb(EM, [128, 512], name="lnb")
        S.dma(lnw_t[:], lnw[:, :], [], [lnw_t])
        S.dma(lnb_t[:], lnb[:, :], [], [lnb_t])
        small = sb(EM, [128, 64], name="small")
        S.dma(small.t[:, 0:8], ldec[:, :], [], [small])
        S.dma(small.t[:, 8:12], k_kT[:, :], [], [small])
        S.dma(small.t[:, 12:16], k_aT[:, :], [], [small])
        S.dma(small.t[:, 16:20], r_kT[:, :], [], [small])
        S.dma(small.t[:, 20:28], w0T.rearrange("p a b -> p (a b)"), [], [small])
        S.dma(small.t[:, 28:36], a0T.rearrange("p a b -> p (a b)"), [], [small])
        mu_t = sb(EM, [128, 14, 2], name="mu")
        mu0 = sb(EM, [128, 14], name="mu0")
        S.dma(mu_t[:], mu[:, :, :], [], [mu_t])
        V("tensor_tensor", [mu_t], [mu0], out=mu0[:], in0=mu_t.t[:, :, 0], in1=mu_t.t[:, :, 1], op=ALU.add)
        V("tensor_scalar", [mu0], [mu0], out=mu0[:], in0=mu0[:], scalar1=-1.0, scalar2=1.0, op0=ALU.mult, op1=ALU.add)
        lgs = sb(EM, [128, 8], name="lgs")
        g128 = sb(EM, [128, 8], name="g128")
        A("activation", [small], [lgs], out=lgs[:], in_=small.t[:, 0:8], func=AF.Exp)
        V("tensor_scalar", [lgs], [lgs], out=lgs[:], in0=lgs[:], scalar1=-1.0, scalar2=None, op0=ALU.mult)
        A("activation", [lgs], [g128], out=g128[:], in_=lgs[:], func=AF.Exp, scale=128.0)
        lup = sb(EM, [128, 2, 512], name="lup")
        for d in range(2):
            S.dma(lup.t[0:64, d, :], w_up[d, :, :], [], [lup])
            S.dma(lup.t[64:128, d, :], a_up[d, :, :], [], [lup])
        gup_t = sb(EM, [128, 512], name="gup")
        S.dma(gup_t[:], g_up[:, :], [], [gup_t])

        for b in range(2):
            with ExitStack() as EA:
                S.barrier()
                xt2 = [sb(EA, [128, D], name="xt%d" % i) for i in range(2)]
                junk = sb(EA, [128, D], name="junk")
                xn = sb(EA, [128, D], name="xn")
                ss = sb(EA, [128, 1], name="ss")
                sd = sb(EA, [128, 1], name="sd")
                rstd = sb(EA, [128, 1], name="rstd")
                for c in range(NCH):
                    j = 2 if c < 2 else b
                    xt = xt2[c % 2]
                    S.dma(xt[:], xs[b, c * 128:(c + 1) * 128, :], [], [xt])
                    rms_rstd(EA, xt, junk, ss, sd, rstd, D, "epsn")
                    V("tensor_scalar", [xt, rstd], [xn], out=xn[:], in0=xt[:], scalar1=rstd.t[:, 0:1], scalar2=None,
                      op0=ALU.mult)
                    for half in range(2):
                        ps = P()
                        for q in range(4):
                            kc = half * 4 + q
                            TR([xn], [ps], out=ps.t[:, q * 128:(q + 1) * 128], in_=xn.t[:, kc * 128:(kc + 1) * 128],
                               identity=ident)
                        for q in range(4):
                            kc = half * 4 + q
                            A("activation", [ps, A1, modT], [nxT.s(c)], out=nxT.t[:, kc, c * 128:(c + 1) * 128],
                              in_=ps.t[:, q * 128:(q + 1) * 128], func=AF.Identity,
                              scale=A1.t[:, kc, j:j + 1], bias=modT.t[:, kc, j:j + 1])

            def proj_fm(wb, evac, ncols=128):
                for tt in range(6):
                    ps = P()
                    rd = [nxT.s(c) for c in range(tt * 3, tt * 3 + 3)]
                    for kc in range(8):
                        MM([wb] + rd, [ps], out=ps.t[:, 0:384], lhsT=wb.t[:, kc, 0:128],
                           rhs=nxT.t[:, kc, tt * 384:(tt + 1) * 384], start=(kc == 0), stop=(kc == 7))
                    evac(ps, tt)

            if b == 0:
                tap('nxT', nxT[:], [128, 8, T], [nxT.s(c_) for c_ in range(NCH)], BF16)
            if stop == 'ret':
                continue
            with ExitStack() as ER:
                S.barrier()
                cosT = sb(ER, [128, T], name="cos")
                sinT = sb(ER, [128, T], name="sin")
                S.dma(cosT[:], cos_d[:, :], [], [cosT])
                S.dma(sinT[:], sin_d[:, :], [], [sinT])
                qT = sb(ER, [128, T], name="qT")
                kT = sb(ER, [128, T], name="kT")
                tmpa = sb(ER, [128, 384], name="tmpa")
                tmpb = sb(ER, [128, 384], name="tmpb")
                vtok = sb(ER, [128, NCH, 128], name="vtok")
                sgt = sb(ER, [128, NCH, 128], name="sgt")
                Sf = sb(ER, [128, NCH, 128], name="Sf")
                Sb_ = sb(ER, [128, NCH, 128], name="Sb")
                UB = sb(ER, [128, NCH, 128], name="UB")
                Dm = sb(ER, [128, 128], name="Dm")
                tB = sb(ER, [128, 128], name="tB")
                qdf = sb(ER, [128, 128], name="qdf")
                qdb = sb(ER, [128, 128], name="qdb")
                kd = sb(ER, [128, 2], name="kd")
                kf_2 = [sb(ER, [128, 128], name="kf%d" % i_) for i_ in range(2)]
                kb_2 = [sb(ER, [128, 128], name="kb%d" % i_) for i_ in range(2)]
                sT_2 = [sb(ER, [128, 128], name="sT%d" % i_) for i_ in range(2)]
                qf_2 = [sb(ER, [128, 128], name="qf%d" % i_) for i_ in range(2)]
                qb_2 = [sb(ER, [128, 128], name="qb%d" % i_) for i_ in range(2)]
                rj_2 = [sb(ER, [128, 128], name="rj%d" % i_) for i_ in range(2)]
                rss_2 = [sb(ER, [128, 1], name="rss%d" % i_) for i_ in range(2)]
                rsd_2 = [sb(ER, [128, 1], name="rsd%d" % i_) for i_ in range(2)]
                rrs_2 = [sb(ER, [128, 1], name="rrs%d" % i_) for i_ in range(2)]
                rtok_2 = [sb(ER, [128, 128], name="rtok%d" % i_) for i_ in range(2)]
                rbf_2 = [sb(ER, [128, 128], BF16, name="rbf%d" % i_) for i_ in range(2)]
                for h in range(4):
                    lf = lgs.t[:, h:h + 1]
                    lb = lgs.t[:, 4 + h:5 + h]
                    A("activation", [lgs], [Dm], out=Dm[:], in_=C("distF"), func=AF.Exp, scale=lf)
                    V("tensor_tensor", [Dm], [Dm], out=Dm[:], in0=Dm[:], in1=C("maskF"), op=ALU.mult)
                    A("activation", [lgs], [tB], out=tB[:], in_=C("distB"), func=AF.Exp, scale=lb)
                    V("tensor_tensor", [tB], [tB], out=tB[:], in0=tB[:], in1=C("maskB"), op=ALU.mult)
                    V("tensor_tensor", [tB, Dm], [Dm], out=Dm[:], in0=Dm[:], in1=tB[:], op=ALU.add)
                    A("activation", [lgs], [qdf], out=qdf[:], in_=C("iota1"), func=AF.Exp, scale=lf)
                    A("activation", [lgs], [qdb], out=qdb[:], in_=C("iotab"), func=AF.Exp, scale=lb)
                    A("activation", [lgs], [kd], out=kd.t[:, 0:1], in_=C("colF"), func=AF.Exp, scale=lf)
                    A("activation", [lgs], [kd], out=kd.t[:, 1:2], in_=C("colB"), func=AF.Exp, scale=lb)
                    chk('r1')
                    for which, dst, c0, p0, scl in (("q", qT, h * 128, h * 128, 1.0),
                                                    ("k", kT, 512 + h * 128, 512 + h * 128, 128.0 ** -0.5)):
                        wa = load_cols(w_in, c0)
                        wp = load_cols(w_perm, p0)
                        for tt in range(6):
                            rd = [nxT.s(c) for c in range(tt * 3, tt * 3 + 3)]
                            pa = P()
                            pb = P()
                            sl = slice(tt * 384, (tt + 1) * 384)
                            for kc in range(8):
                                MM([wa] + rd, [pa], out=pa.t[:, 0:384], lhsT=wa.t[:, kc, :], rhs=nxT.t[:, kc, sl],
                                   start=(kc == 0), stop=(kc == 7))
                            for kc in range(8):
                                MM([wp] + rd, [pb], out=pb.t[:, 0:384], lhsT=wp.t[:, kc, :], rhs=nxT.t[:, kc, sl],
                                   start=(kc == 0), stop=(kc == 7))
                            V("scalar_tensor_tensor", [pa, cosT], [tmpa], out=tmpa[:], in0=pa.t[:, 0:384], scalar=scl,
                              in1=cosT.t[:, sl], op0=ALU.mult, op1=ALU.mult)
                            V("scalar_tensor_tensor", [pb, sinT], [tmpb], out=tmpb[:], in0=pb.t[:, 0:384], scalar=scl,
                              in1=sinT.t[:, sl], op0=ALU.mult, op1=ALU.mult)
                            G("tensor_tensor", [tmpa, tmpb], [dst.s(tt)], out=dst.t[:, sl], in0=tmpa[:], in1=tmpb[:],
                              op=ALU.add)
                    chk('r2')
                    wv = load_cols(w_in, 1024 + h * 128)
                    wg = load_cols(w_in, 1536 + h * 128)
                    chk('r2a')
                    for c in range(NCH):
                        if c == 1:
                            chk('r2c')
                        ps = P()
                        for kc in range(8):
                            MM([wv, nxT.s(c)], [ps], out=ps.t[:, 0:128], lhsT=nxT.t[:, kc, c * 128:(c + 1) * 128],
                               rhs=wv.t[:, kc, :], start=(kc == 0), stop=(kc == 7))
                        for kc in range(8):
                            MM([wg, nxT.s(c)], [ps], out=ps.t[:, 128:256], lhsT=nxT.t[:, kc, c * 128:(c + 1) * 128],
                               rhs=wg.t[:, kc, :], start=(kc == 0), stop=(kc == 7))
                        _sk = ''
                        if not (_sk == 'v' and c >= 1):
                            V("tensor_copy", [ps], [vtok.s(c)], out=vtok.t[:, c, :], in_=ps.t[:, 0:128])
                        if not (_sk == 'a' and c >= 1):
                            A("activation", [ps], [sgt.s(c)], out=sgt.t[:, c, :], in_=ps.t[:, 128:256], func=AF.Silu)
                    chk('r3')
                    V("memset", [], [Sf.s(0)], ap=Sf.t[:, 0, :], constant=0.0)
                    V("memset", [], [Sb_.s(1)], ap=Sb_.t[:, 1, :], constant=0.0)
                    for c in range(NCH):
                        kf, kb = kf_2[c % 2], kb_2[c % 2]
                        ps = P()
                        TR([kT.s(c // 3)], [ps], out=ps.t[:, 0:128], in_=kT.t[:, c * 128:(c + 1) * 128], identity=ident)
                        V("tensor_scalar", [ps, kd], [kf], out=kf[:], in0=ps.t[:, 0:128], scalar1=kd.t[:, 0:1],
                          scalar2=None, op0=ALU.mult)
                        V("tensor_scalar", [ps, kd], [kb], out=kb[:], in0=ps.t[:, 0:128], scalar1=kd.t[:, 1:2],
                          scalar2=None, op0=ALU.mult)
                        pu = P()
                        MM([kf, vtok.s(c)], [pu], out=pu.t[:, 0:128], lhsT=kf[:], rhs=vtok.t[:, c, :], start=True, stop=True)
                        MM([kb, vtok.s(c)], [pu], out=pu.t[:, 128:256], lhsT=kb[:], rhs=vtok.t[:, c, :], start=True, stop=True)
                        if c + 1 < NCH:
                            V("scalar_tensor_tensor", [pu, Sf.s(c), g128], [Sf.s(c + 1)], out=Sf.t[:, c + 1, :],
                              in0=Sf.t[:, c, :], scalar=g128.t[:, h:h + 1], in1=pu.t[:, 0:128], op0=ALU.mult, op1=ALU.add)
                        A("activation", [pu], [UB.s(c)], out=UB.t[:, c, :], in_=pu.t[:, 128:256], func=AF.Copy)
                    border = [1, 0] + list(range(17, 1, -1))
                    for a_, b_ in zip(border[:-1], border[1:]):
                        V("scalar_tensor_tensor", [UB.s(a_), Sb_.s(a_), g128], [Sb_.s(b_)], out=Sb_.t[:, b_, :],
                          in0=Sb_.t[:, a_, :], scalar=g128.t[:, 4 + h:5 + h], in1=UB.t[:, a_, :], op0=ALU.mult, op1=ALU.add)
                    if b == 0 and h == 0:
                        tap('qT', qT[:], [128, T], [qT.s(i_) for i_ in range(6)])
                        tap('cos', cosT[:], [128, T], [cosT])
                        tap('sin', sinT[:], [128, T], [sinT])
                        tap('nxT2', nxT[:], [128, 8, T], [nxT.s(c_) for c_ in range(NCH)], BF16)
                        pass
                        tap('kT', kT[:], [128, T], [kT.s(i_) for i_ in range(6)])
                        tap('vtok', vtok[:], [128, NCH, 128], [vtok.s(i_) for i_ in range(NCH)])
                        tap('sgt', sgt[:], [128, NCH, 128], [sgt.s(i_) for i_ in range(NCH)])
                        tap('Sf', Sf[:], [128, NCH, 128], [Sf.s(i_) for i_ in range(NCH)])
                        tap('Sb', Sb_[:], [128, NCH, 128], [Sb_.s(i_) for i_ in range(NCH)])
                        tap('Dm', Dm[:], [128, 128], [Dm])
                        tap('qdf', qdf[:], [128, 128], [qdf])
                        tap('qdb', qdb[:], [128, 128], [qdb])
                        tap('kd', kd[:], [128, 2], [kd])
                        tap('lgs', lgs[:], [128, 8], [lgs])
                        tap('g128', g128[:], [128, 8], [g128])
                    chk('r4')
                    for c in range(2, NCH):
                        sT, qf, qb, rj, rss, rsd, rrs, rtok, rbf = (sT_2[c % 2], qf_2[c % 2], qb_2[c % 2], rj_2[c % 2], rss_2[c % 2],
                                                                     rsd_2[c % 2], rrs_2[c % 2], rtok_2[c % 2], rbf_2[c % 2])
                        sl = slice(c * 128, (c + 1) * 128)
                        ps = P()
                        MM([kT.s(c // 3), qT.s(c // 3)], [ps], out=ps.t[:, 0:128], lhsT=kT.t[:, sl], rhs=qT.t[:, sl],
                           start=True, stop=True)
                        V("tensor_tensor", [ps, Dm], [sT], out=sT[:], in0=ps.t[:, 0:128], in1=Dm[:], op=ALU.mult)
                        G("tensor_tensor", [qT.s(c // 3), qdf], [qf], out=qf[:], in0=qT.t[:, sl], in1=qdf[:], op=ALU.mult)
                        G("tensor_tensor", [qT.s(c // 3), qdb], [qb], out=qb[:], in0=qT.t[:, sl], in1=qdb[:], op=ALU.mult)
                        po = P()
                        MM([sT, vtok.s(c)], [po], out=po.t[:, 0:128], lhsT=sT[:], rhs=vtok.t[:, c, :], start=True, stop=False)
                        MM([qf, Sf.s(c)], [po], out=po.t[:, 0:128], lhsT=qf[:], rhs=Sf.t[:, c, :], start=False, stop=False)
                        MM([qb, Sb_.s(c)], [po], out=po.t[:, 0:128], lhsT=qb[:], rhs=Sb_.t[:, c, :], start=False, stop=True)
                        A("activation", [po], [rj], out=rj[:], in_=po.t[:, 0:128], func=AF.Square)
                        V("tensor_reduce", [rj], [rss], out=rss[:], in_=rj[:], axis=AX.X, op=ALU.add)
                        A("activation", [rss], [rsd], out=rsd[:], in_=rss[:], func=AF.Sqrt, scale=1.0 / 128, bias=C("epsn"))
                        V("reciprocal", [rsd], [rrs], out=rrs[:], in_=rsd[:])
                        V("scalar_tensor_tensor", [po, rrs, sgt.s(c)], [rtok], out=rtok[:], in0=po.t[:, 0:128],
                          scalar=rrs.t[:, 0:1], in1=sgt.t[:, c, :], op0=ALU.mult, op1=ALU.mult)
                        pt = P()
                        TR([rtok], [pt], out=pt.t[:, 0:128], in_=rtok[:], identity=ident)
                        A("activation", [pt], [rbf], out=rbf[:], in_=pt.t[:, 0:128], func=AF.Copy)
                        S.dma(mixs[b, h, :, (c - 2) * 128:(c - 1) * 128], rbf[:], [rbf], [mix_dep[b][c - 2]])

            if stop == 'rwkv':
                continue
            with ExitStack() as EW:
                S.barrier()
                LT = sb(EW, [128, T], name="LT")
                sgG = sb(EW, [128, T], name="sgG")
                rT = sb(EW, [128, T], name="rT")
                kT = sb(EW, [128, T], name="kTw")
                kkT = sb(EW, [128, T], name="kkT")
                vtok = sb(EW, [128, NCH, 128], name="vtokw")
                ytot = sb(EW, [128, 16, 128], name="ytot")
                bsum = sb(EW, [128, NCH, 2], name="bsum")
                st2_2 = [sb(EW, [128, 8], name="st2_%d" % i_) for i_ in range(2)]
                rbf2_2 = [sb(EW, [128, 128], BF16, name="rbf2_%d" % i_) for i_ in range(2)]
                yn_2 = [sb(EW, [128, 128], name="yn%d" % i_) for i_ in range(2)]
                sqy_2 = [sb(EW, [128, 128], name="sqy%d" % i_) for i_ in range(2)]

                def shift_proj(g, c0, dst, raw):
                    wb = load_cols(w_in, c0)

                    def ev(ps, tt):
                        A("activation", [ps], [raw.s(tt)], out=raw.t[:, tt * 384:(tt + 1) * 384], in_=ps.t[:, 0:384],
                          func=AF.Copy)
                    proj_fm(wb, ev)
                    allraw = [raw.s(tt) for tt in range(6)]
                    for (s0, s1) in ((0, 256), (256, T)):
                        V("tensor_scalar", allraw + [mu0], [dst], out=dst.t[:, s0:s1], in0=raw.t[:, s0:s1],
                          scalar1=mu0.t[:, g:g + 1], scalar2=None, op0=ALU.mult)
                        V("scalar_tensor_tensor", allraw + [mu_t, dst], [dst], out=dst.t[:, s0 + 1:s1],
                          in0=raw.t[:, s0:s1 - 1], scalar=mu_t.t[:, g, 0:1], in1=dst.t[:, s0 + 1:s1],
                          op0=ALU.mult, op1=ALU.add)
                        V("scalar_tensor_tensor", allraw + [mu_t, dst], [dst], out=dst.t[:, s0:s1 - 1],
                          in0=raw.t[:, s0 + 1:s1], scalar=mu_t.t[:, g, 1:2], in1=dst.t[:, s0:s1 - 1],
                          op0=ALU.mult, op1=ALU.add)

                with ExitStack() as EP:
                    raw = sb(EP, [128, T], name="raw")
                    shift_proj(12, 2048 + 1536, LT, raw)
                    A("activation", [LT], [LT], out=LT.t[0:64, :], in_=LT.t[0:64, :], func=AF.Tanh)
                    shift_proj(13, 2048 + 1664, sgG, raw)
                    A("activation", [sgG], [sgG], out=sgG[:], in_=sgG[:], func=AF.Sigmoid)
                for hp in range(4):
                    kcol = small.t[:, 8 + hp:9 + hp]
                    kacol = small.t[:, 12 + hp:13 + hp]
                    rkcol = small.t[:, 16 + hp:17 + hp]
                    with ExitStack() as EP:
                        S.barrier()
                        raw = sb(EP, [128, T], name="raw")
                        vT = sb(EP, [128, T], name="vT")
                        sq = sb(EP, [128, 384], name="sq")
                        sq2 = sb(EP, [128, 384], name="sq2")
                        shift_proj(hp, 2048 + hp * 128, rT, raw)
                        shift_proj(4 + hp, 2048 + 512 + hp * 128, kT, raw)
                        shift_proj(8 + hp, 2048 + 1024 + hp * 128, vT, raw)
                        for c in range(NCH):
                            ps = P()
                            TR([vT], [ps], out=ps.t[:, 0:128], in_=vT.t[:, c * 128:(c + 1) * 128], identity=ident)
                            A("activation", [ps], [vtok.s(c)], out=vtok.t[:, c, :], in_=ps.t[:, 0:128], func=AF.Copy)
                        V("tensor_scalar", [kT, small], [kkT], out=kkT[:], in0=kT[:], scalar1=kcol, scalar2=None, op0=ALU.mult)
                        for tt in range(6):
                            sl = slice(tt * 384, (tt + 1) * 384)
                            G("tensor_tensor", [kkT], [sq], out=sq[:], in0=kkT.t[:, sl], in1=kkT.t[:, sl], op=ALU.mult)
                            ps = P()
                            MM([sq], [ps], out=ps.t[:, 0:384], lhsT=C("bones"), rhs=sq[:], start=True, stop=True)
                            A("activation", [ps], [sq2], out=sq2[:], in_=ps.t[:, 0:384], func=AF.Sqrt, bias=C("epsk"), scale=1.0)
                            V("reciprocal", [sq2], [sq2], out=sq2[:], in_=sq2[:])
                            V("tensor_tensor", [kkT, sq2], [kkT], out=kkT.t[:, sl], in0=kkT.t[:, sl], in1=sq2[:], op=ALU.mult)
                        for c in range(2, NCH):
                            G("memset", [], [ytot.s(c)], ap=ytot.t[:, c - 2, :], constant=0.0)

                    with ExitStack() as ED:
                        S.barrier()
                        if b == 0 and hp == 0:
                            r_ = nc.sbuf_bytes_remaining
                            print('SBUF remaining before ED:', r_() if callable(r_) else r_)
                        XD = []
                        for d in range(2):
                            X_ = {}
                            for n_ in ("sg", "aa", "pin", "cx", "cy", "Ek", "Ei", "Er", "Ec", "be", "kka", "kt", "bti", "kti",
                                       "Bm", "Kh", "prod", "GG0", "GG1", "UUn", "PhiT", "tmpP", "Y1T", "Y0s", "sgt_", "Bte0", "Bte1",
                                       "Of_0", "OfT_0",
                                       "Of_1", "OfT_1",
                                       "Zt_0", "Zt_1"):
                                if (d == 0 and n_ == "cy") or (d == 1 and n_ in ("Er", "prod")):
                                    continue
                                X_[n_] = sb(ED, [128, 128], name=n_ + "d%d" % d)
                            X_["KR"] = sb(ED, [128, 256], name="KR%d" % d)
                            for e_ in range(2):
                                for n_ in ("PP0", "PP1", "TX0", "TX1", "VW"):
                                    X_["%s_%d" % (n_, e_)] = sb(ED, [128, 256], name="%s_%d_%d" % (n_, e_, d))
                            X_["AR"] = [sb(ED, [128, 256], name="AR%d_%d" % (e, d)) for e in range(2)]
                            X_["NR"] = [sb(ED, [128, 256], name="NR%d_%d" % (e, d)) for e in range(2)]
                            X_["TK"] = sb(ED, [128, 384], name="TK%d" % d)
                            X_["Hs"] = [sb(ED, [128, 64], name="H%d_%d" % (i, d)) for i in range(2)]
                            X_["Psi"] = sb(ED, [128, 64], name="Psi%d" % d)
                            X_["cols"] = sb(ED, [128, 8], name="cols%d" % d)
                            V("memset", [], [X_["GG0"]], ap=X_["GG0"][:], constant=0.0)
                            V("memset", [], [X_["GG1"]], ap=X_["GG1"][:], constant=0.0)
                            XD.append(X_)

                        def dir_chain(d, X_):
                            order = list(range(NCH)) if d == 0 else [1, 0] + list(range(17, 1, -1))
                            sfx = "f" if d == 0 else "b"
                            mk1, mk2, mk3 = C("mk1" + sfx), C("mk2" + sfx), C("mk3" + sfx)
                            w0c = small.t[:, 20 + hp * 2 + d:21 + hp * 2 + d]
                            a0c = small.t[:, 28 + hp * 2 + d:29 + hp * 2 + d]
                            KR, AR, NR, TK, Hs, Psi, cols = X_["KR"], X_["AR"], X_["NR"], X_["TK"], X_["Hs"], X_["Psi"], X_["cols"]
                            hi = 0
                            V("memset", [], [Hs[0]], ap=Hs[0][:], constant=0.0)
                            for c in order:
                                sl = slice(c * 128, (c + 1) * 128)
                                ps = P()
                                MM([lup, LT], [ps], out=ps.t[:, 0:128], lhsT=lup.t[0:64, d, hp * 128:(hp + 1) * 128],
                                   rhs=LT.t[0:64, sl], start=True, stop=True)
                                ps2 = P()
                                MM([lup, LT], [ps2], out=ps2.t[:, 128:256], lhsT=lup.t[64:128, d, hp * 128:(hp + 1) * 128],
                                   rhs=LT.t[64:128, sl], start=True, stop=True)
                                sg, aa, pin, cx, cy = X_["sg"], X_["aa"], X_["pin"], X_["cx"], X_.get("cy")
                                A("activation", [ps, small], [sg], out=sg[:], in_=ps.t[:, 0:128], func=AF.Sigmoid, bias=w0c, scale=1.0)
                                A("activation", [ps2, small], [aa], out=aa[:], in_=ps2.t[:, 128:256], func=AF.Sigmoid, bias=a0c, scale=1.0)
                                yield
                                pc_ = P()
                                TR([sg], [pc_], out=pc_.t[:, 0:128], in_=sg[:], identity=ident)
                                V("tensor_copy", [pc_], [X_["sgt_"]], out=X_["sgt_"][:], in_=pc_.t[:, 0:128])
                                yield
                                pc2 = P()
                                MM([X_["sgt_"]], [pc2], out=pc2.t[:, 0:128], lhsT=X_["sgt_"][:], rhs=cst.t[:, CST["mk2f"][0] + 128:CST["mk2f"][0] + 256],
                                   start=True, stop=True)
                                V("tensor_copy", [pc2], [pin], out=pin[:], in_=pc2.t[:, 0:128])
                                tot = pin.t[:, 127:128]
                                if d == 0:
                                    V("tensor_tensor", [pin, sg], [cx], out=cx[:], in0=pin[:], in1=sg[:], op=ALU.subtract)
                                    ci, ce = pin, cx
                                else:
                                    V("tensor_scalar", [pin], [cx], out=cx[:], in0=pin[:], scalar1=-1.0, scalar2=tot,
                                      op0=ALU.mult, op1=ALU.add)
                                    V("tensor_tensor", [cx, sg], [cy], out=cy[:], in0=cx[:], in1=sg[:], op=ALU.add)
                                    ci, ce = cy, cx
                                V("tensor_scalar", [pin], [cols], out=cols.t[:, 0:3], in0=C("cv3"), scalar1=tot, scalar2=None, op0=ALU.mult)
                                yield
                                A("activation", [cols], [cols], out=cols.t[:, 3:5], in_=cols.t[:, 1:3], func=AF.Exp)
                                V("tensor_scalar", [cols], [cols], out=cols.t[:, 5:6], in0=cols.t[:, 3:4], scalar1=-1.0, scalar2=None, op0=ALU.mult)
                                Qc, PCc, nQc = cols.t[:, 3:4], cols.t[:, 4:5], cols.t[:, 5:6]
                                Ek, Ei, Er, Ec = X_["Ek"], X_["Ei"], X_.get("Er"), X_["Ec"]
                                A("activation", [ce, cols], [Ek], out=Ek[:], in_=ce[:], func=AF.Exp, scale=-SW, bias=cols.t[:, 0:1])
                                A("activation", [ci, cols], [Ei], out=Ei[:], in_=ci[:], func=AF.Exp, scale=SW, bias=cols.t[:, 1:2])
                                if d == 0:
                                    A("activation", [ci, cols], [Er], out=Er[:], in_=ci[:], func=AF.Exp, scale=-SW, bias=cols.t[:, 0:1])
                                    Eru = Er
                                else:
                                    Eru = Ek
                                A("activation", [ci, cols], [Ec], out=Ec[:], in_=ci[:], func=AF.Exp, scale=SW, bias=cols.t[:, 2:3])
                                be, kka, kt = X_["be"], X_["kka"], X_["kt"]
                                V("tensor_tensor", [aa, kkT], [be], out=be[:], in0=aa[:], in1=kkT.t[:, sl], op=ALU.mult)
                                V("tensor_scalar", [kT, small], [kka], out=kka[:], in0=kT.t[:, sl], scalar1=kacol, scalar2=None, op0=ALU.mult)
                                V("scalar_tensor_tensor", [aa, kka], [kt], out=kt[:], in0=aa[:], scalar=-1.0, in1=kka[:],
                                  op0=ALU.add, op1=ALU.mult)
                                G("tensor_tensor", [kt, kT], [kt], out=kt[:], in0=kt[:], in1=kT.t[:, sl], op=ALU.add)
                                yield
                                G("tensor_tensor", [kkT, Ek], [KR.s(0)], out=KR.t[:, 0:128], in0=kkT.t[:, sl], in1=Ek[:], op=ALU.mult)
                                V("tensor_tensor", [rT, Eru], [KR.s(1)], out=KR.t[:, 128:256], in0=rT.t[:, sl], in1=Eru[:], op=ALU.mult)
                                bti, kti, Bm, Kh = X_["bti"], X_["kti"], X_["Bm"], X_["Kh"]
                                V("tensor_tensor", [be, Ei], [bti], out=bti[:], in0=be[:], in1=Ei[:], op=ALU.mult)
                                G("tensor_tensor", [kt, Ei], [kti], out=kti[:], in0=kt[:], in1=Ei[:], op=ALU.mult)
                                V("tensor_tensor", [be, Ec], [Bm], out=Bm[:], in0=be[:], in1=Ec[:], op=ALU.mult)
                                G("tensor_tensor", [kt, Ec], [Kh], out=Kh[:], in0=kt[:], in1=Ec[:], op=ALU.mult)
                                if d == 0:
                                    prod = X_["prod"]
                                    V("scalar_tensor_tensor", [rT, small, kt], [prod], out=prod[:], in0=rT.t[:, sl], scalar=rkcol,
                                      in1=kt[:], op0=ALU.mult, op1=ALU.mult)
                                    pb_ = P()
                                    MM([prod], [pb_], out=pb_.t[:, 0:2], lhsT=prod[:], rhs=C("bcols"), start=True, stop=True)
                                    A("activation", [pb_], [bsum.s(c)], out=bsum.t[:, c, :], in_=pb_.t[:, 0:2], func=AF.Copy)
                                yield
                                KRd = [KR.s(0), KR.s(1)]
                                pt = P()
                                TR([KR.s(0)], [pt], out=pt.t[:, 0:128], in_=KR.t[:, 0:128], identity=ident)
                                TR([Bm], [pt], out=pt.t[:, 128:256], in_=Bm[:], identity=ident)
                                TR([Kh], [pt], out=pt.t[:, 256:384], in_=Kh[:], identity=ident)
                                A("activation", [pt], [TK], out=TK[:], in_=pt.t[:, 0:384], func=AF.Copy)
                                UUn = X_["UUn"]
                                GG = [X_["GG0"], X_["GG1"]]
                                PhiT, tmpP, Y1T, Y0s = X_["PhiT"], X_["tmpP"], X_["Y1T"], X_["Y0s"]

                                def chain(e):
                                    er = slice(e * 64, (e + 1) * 64)
                                    p1 = P()
                                    MM([bti] + KRd, [p1], out=p1.t[:, 0:256], lhsT=bti.t[er, :], rhs=KR.t[er, :], start=True, stop=True)
                                    MM([bti] + KRd, [p1], out=p1.t[:, 256:384], lhsT=KR.t[er, 0:128], rhs=bti.t[er, :], start=True, stop=True)
                                    yield
                                    p2 = P()
                                    MM([kti] + KRd, [p2], out=p2.t[:, 0:256], lhsT=kti.t[er, :], rhs=KR.t[er, :], start=True, stop=True)
                                    Bt0 = X_["Bte%d" % e]
                                    V("tensor_tensor", [p1], [AR[e]], out=AR[e][:], in0=p1.t[:, 0:256], in1=mk1, op=ALU.mult)
                                    V("tensor_tensor", [p1], [Bt0], out=Bt0[:], in0=p1.t[:, 256:384], in1=mk3, op=ALU.mult)
                                    V("tensor_tensor", [p2], [NR[e]], out=NR[e][:], in0=p2.t[:, 0:256], in1=mk2, op=ALU.mult)
                                    A_ap = AR[e].t[:, 0:128]
                                    PP = [X_["PP0_%d" % e], X_["PP1_%d" % e]]
                                    TX = [X_["TX0_%d" % e], X_["TX1_%d" % e]]
                                    VW = X_["VW_%d" % e]
                                    G("tensor_tensor", [Bt0], [PP[0]], out=PP[0].t[:, 0:128], in0=Bt0[:], in1=C("blk16"), op=ALU.mult)
                                    G("tensor_tensor", [AR[e]], [PP[0]], out=PP[0].t[:, 128:256], in0=A_ap, in1=C("blk16"), op=ALU.mult)
                                    V("tensor_tensor", [PP[0]], [TX[0]], out=TX[0][:], in0=PP[0][:], in1=C("ident2"), op=ALU.add)
                                    pi = 0
                                    ti = 0
                                    for s_ in range(3):
                                        yield
                                        pq = P()
                                        MM([PP[pi]], [pq], out=pq.t[:, 0:128], lhsT=PP[pi].t[:, 128:256], rhs=PP[pi].t[:, 0:128], start=True, stop=True)
                                        MM([PP[pi]], [pq], out=pq.t[:, 128:256], lhsT=PP[pi].t[:, 0:128], rhs=PP[pi].t[:, 128:256], start=True, stop=True)
                                        A("activation", [pq], [PP[1 - pi]], out=PP[1 - pi][:], in_=pq.t[:, 0:256], func=AF.Copy)
                                        pi = 1 - pi
                                        yield
                                        px = P()
                                        MM([PP[pi], TX[ti]], [px], out=px.t[:, 0:128], lhsT=PP[pi].t[:, 128:256], rhs=TX[ti].t[:, 0:128], start=True, stop=True)
                                        MM([PP[pi], TX[ti]], [px], out=px.t[:, 128:256], lhsT=PP[pi].t[:, 0:128], rhs=TX[ti].t[:, 128:256], start=True, stop=True)
                                        V("tensor_tensor", [px, TX[ti]], [TX[1 - ti]], out=TX[1 - ti][:], in0=px.t[:, 0:256], in1=TX[ti][:], op=ALU.add)
                                        ti = 1 - ti
                                    for lvl, on in enumerate(("o16", "o32", "o64")):
                                        lastl = (lvl == 2)
                                        Of, OfT = X_["Of_%d" % e], X_["OfT_%d" % e]
                                        G("tensor_tensor", [Bt0], [Of], out=Of[:], in0=Bt0[:], in1=C(on), op=ALU.mult)
                                        if not lastl:
                                            G("tensor_tensor", [AR[e]], [OfT], out=OfT[:], in0=A_ap, in1=C(on), op=ALU.mult)
                                        yield
                                        pv = P()
                                        MM([Of, TX[ti]], [pv], out=pv.t[:, 128:256], lhsT=Of[:], rhs=TX[ti].t[:, 128:256], start=True, stop=True)
                                        if not lastl:
                                            MM([OfT, TX[ti]], [pv], out=pv.t[:, 0:128], lhsT=OfT[:], rhs=TX[ti].t[:, 0:128], start=True, stop=True)
                                            A("activation", [pv], [VW], out=VW[:], in_=pv.t[:, 0:256], func=AF.Copy)
                                        else:
                                            A("activation", [pv], [VW], out=VW.t[:, 128:256], in_=pv.t[:, 128:256], func=AF.Copy)
                                        yield
                                        px = P()
                                        MM([TX[ti], VW], [px], out=px.t[:, 128:256], lhsT=TX[ti].t[:, 0:128], rhs=VW.t[:, 128:256], start=True, stop=True)
                                        if not lastl:
                                            MM([TX[ti], VW], [px], out=px.t[:, 0:128], lhsT=TX[ti].t[:, 128:256], rhs=VW.t[:, 0:128], start=True, stop=True)
                                            V("tensor_tensor", [px, TX[ti]], [TX[1 - ti]], out=TX[1 - ti][:], in0=px.t[:, 0:256], in1=TX[ti][:], op=ALU.add)
                                        else:
                                            V("tensor_tensor", [px, TX[ti]], [TX[1 - ti]], out=TX[1 - ti].t[:, 128:256], in0=px.t[:, 128:256], in1=TX[ti].t[:, 128:256], op=ALU.add)
                                        ti = 1 - ti
                                    Xc = TX[ti]
                                    Zt = X_["Zt_%d" % e]
                                    G("tensor_copy", [TK], [Zt.s(0)], out=Zt.t[:, 0:64], in_=TK.t[:, e * 64:(e + 1) * 64])
                                    yield
                                    pn = P()
                                    MM([NR[e], vtok.s(c)], [pn], out=pn.t[:, 0:64], lhsT=NR[e].t[:, 0:128], rhs=vtok.t[:, c, er], start=True, stop=True)
                                    A("activation", [pn], [Zt.s(1)], out=Zt.t[:, 64:128], in_=pn.t[:, 0:64], func=AF.Copy)
                                    yield
                                    pg = P()
                                    MM([Xc, Zt.s(0), Zt.s(1)], [pg], out=pg.t[:, 0:128], lhsT=Xc.t[:, 128:256], rhs=Zt[:], start=True, stop=True)
                                    A("activation", [pg], [GG[e]], out=GG[e].t[:, er], in_=pg.t[:, 0:64], func=AF.Copy)
                                    V("tensor_scalar", [pg], [UUn.s(e)], out=UUn.t[:, er], in0=pg.t[:, 64:128], scalar1=-1.0, scalar2=None, op0=ALU.mult)
                                    yield
                                    pY0 = P()
                                    MM([NR[e], vtok.s(c)], [pY0], out=pY0.t[:, er], lhsT=NR[e].t[:, 128:256], rhs=vtok.t[:, c, er], start=True, stop=False)
                                    MM([AR[e], UUn.s(e)], [pY0], out=pY0.t[:, er], lhsT=AR[e].t[:, 128:256], rhs=UUn.t[:, er], start=False, stop=True)
                                    pYP = P()
                                    MM([GG[e], AR[e]], [pYP], out=pYP.t[:, 0:128], lhsT=GG[e][:], rhs=AR[e].t[:, 128:256], start=True, stop=True)
                                    MM([GG[e], TK], [pYP], out=pYP.t[:, 128:256], lhsT=GG[e][:], rhs=TK.t[:, 128:256], start=True, stop=True)
                                    A("activation", [pY0], [Y0s.s(e)], out=Y0s.t[:, er], in_=pY0.t[:, er], func=AF.Copy)
                                    V("tensor_tensor", [KR.s(1), pYP], [Y1T.s(e)], out=Y1T.t[er, :], in0=KR.t[er, 128:256], in1=pYP.t[er, 0:128], op=ALU.subtract)
                                    V("tensor_scalar", [Y1T.s(e), cols], [Y1T.s(e)], out=Y1T.t[er, :], in0=Y1T.t[er, :], scalar1=cols.t[er, 3:4], scalar2=None, op0=ALU.mult)
                                    V("scalar_tensor_tensor", [pYP, cols], [tmpP.s(e)], out=tmpP.t[er, :], in0=pYP.t[er, 128:256], scalar=cols.t[er, 5:6],
                                      in1=cst.t[er, CST["bones"][0]:CST["bones"][0] + 128], op0=ALU.mult, op1=ALU.mult)
                                    V("scalar_tensor_tensor", [tmpP.s(e), cols], [PhiT.s(e)], out=PhiT.t[er, :], in0=cst.t[er, CST["ident"][0]:CST["ident"][0] + 128],
                                      scalar=cols.t[er, 4:5], in1=tmpP.t[er, :], op0=ALU.mult, op1=ALU.add)

                                gens = [chain(0), chain(1)]
                                while gens:
                                    for g_ in list(gens):
                                        try:
                                            next(g_)
                                        except StopIteration:
                                            gens.remove(g_)
                                    yield
                                pPsi = P()
                                MM([TK, vtok.s(c)], [pPsi], out=pPsi.t[:, 0:128], lhsT=TK.t[:, 256:384], rhs=vtok.t[:, c, :], start=True, stop=False)
                                MM([TK, UUn.s(0), UUn.s(1)], [pPsi], out=pPsi.t[:, 0:128], lhsT=TK.t[:, 128:256], rhs=UUn[:], start=False, stop=True)
                                A("activation", [pPsi], [Psi.s(0)], out=Psi.t[0:64, :], in_=pPsi.t[0:64, 0:64], func=AF.Copy)
                                A("activation", [pPsi], [Psi.s(1)], out=Psi.t[64:128, :], in_=pPsi.t[64:128, 64:128], func=AF.Copy)
                                if c >= 2:
                                    pyy2 = [P(), P()]
                                    for e in range(2):
                                        er = slice(e * 64, (e + 1) * 64)
                                        MM([Y1T.s(e), Hs[hi]], [pyy2[e]], out=pyy2[e].t[:, er], lhsT=Y1T.t[er, :], rhs=Hs[hi].t[er, :], start=True, stop=True)
                                    for e in range(2):
                                        er = slice(e * 64, (e + 1) * 64)
                                        V("tensor_tensor", [pyy2[e], Y0s.s(e)], [Y0s.s(e)], out=Y0s.t[:, er], in0=pyy2[e].t[:, er], in1=Y0s.t[:, er], op=ALU.add)
                                    G("tensor_tensor", [Y0s.s(0), Y0s.s(1), ytot.s(c)], [ytot.s(c)], out=ytot.t[:, c - 2, :], in0=ytot.t[:, c - 2, :], in1=Y0s[:], op=ALU.add)
                                pH = P()
                                MM([PhiT.s(0), PhiT.s(1), Hs[hi]], [pH], out=pH.t[:, 0:64], lhsT=PhiT[:], rhs=Hs[hi][:], start=True, stop=True)
                                V("tensor_tensor", [pH, Psi.s(0), Psi.s(1)], [Hs[1 - hi]], out=Hs[1 - hi][:], in0=pH.t[:, 0:64], in1=Psi[:], op=ALU.add)
                                hi = 1 - hi
                                yield

                        dgens = [dir_chain(0, XD[0]), dir_chain(1, XD[1])]
                        while dgens:
                            for g_ in list(dgens):
                                try:
                                    next(g_)
                                except StopIteration:
                                    dgens.remove(g_)

                    S.barrier()
                    for c in range(2, NCH):
                        st2, rbf2, yn, sqy = st2_2[c % 2], rbf2_2[c % 2], yn_2[c % 2], sqy_2[c % 2]
                        sl = slice(c * 128, (c + 1) * 128)
                        yc = ytot.t[:, c - 2, :]
                        G("tensor_tensor", [ytot.s(c)], [sqy], out=sqy[:], in0=yc, in1=yc, op=ALU.mult)
                        for e in range(2):
                            er = slice(e * 64, (e + 1) * 64)
                            V("tensor_reduce", [ytot.s(c)], [st2], out=st2.t[:, e:e + 1], in_=ytot.t[:, c - 2, er], axis=AX.X, op=ALU.add)
                            V("tensor_reduce", [sqy], [st2], out=st2.t[:, 2 + e:3 + e], in_=sqy.t[:, er], axis=AX.X, op=ALU.add)
                        V("tensor_scalar", [st2], [st2], out=st2.t[:, 0:2], in0=st2.t[:, 0:2], scalar1=1.0 / 64, scalar2=None, op0=ALU.mult)
                        V("tensor_tensor", [st2], [st2], out=st2.t[:, 4:6], in0=st2.t[:, 0:2], in1=st2.t[:, 0:2], op=ALU.mult)
                        V("scalar_tensor_tensor", [st2], [st2], out=st2.t[:, 6:8], in0=st2.t[:, 2:4], scalar=1.0 / 64, in1=st2.t[:, 4:6],
                          op0=ALU.mult, op1=ALU.subtract)
                        A("activation", [st2], [st2], out=st2.t[:, 2:4], in_=st2.t[:, 6:8], func=AF.Sqrt, bias=C("epsg"), scale=1.0)
                        V("reciprocal", [st2], [st2], out=st2.t[:, 4:6], in_=st2.t[:, 2:4])
                        for e in range(2):
                            er = slice(e * 64, (e + 1) * 64)
                            V("tensor_scalar", [ytot.s(c), st2], [yn], out=yn.t[:, er], in0=ytot.t[:, c - 2, er], scalar1=st2.t[:, e:e + 1],
                              scalar2=st2.t[:, 4 + e:5 + e], op0=ALU.subtract, op1=ALU.mult)
                        G("tensor_tensor", [yn, lnw_t], [yn], out=yn[:], in0=yn[:], in1=lnw_t.t[:, hp * 128:(hp + 1) * 128], op=ALU.mult)
                        G("tensor_tensor", [yn, lnb_t], [yn], out=yn[:], in0=yn[:], in1=lnb_t.t[:, hp * 128:(hp + 1) * 128], op=ALU.add)
                        for e in range(2):
                            er = slice(e * 64, (e + 1) * 64)
                            V("scalar_tensor_tensor", [vtok.s(c), bsum.s(c), yn], [yn], out=yn.t[:, er], in0=vtok.t[:, c, er],
                              scalar=bsum.t[:, c, e:e + 1], in1=yn.t[:, er], op0=ALU.mult, op1=ALU.add)
                        pg = P()
                        MM([sgG, gup_t], [pg], out=pg.t[:, 0:128], lhsT=sgG.t[:, sl], rhs=gup_t.t[:, hp * 128:(hp + 1) * 128], start=True, stop=True)
                        V("tensor_tensor", [pg, yn], [yn], out=yn[:], in0=pg.t[:, 0:128], in1=yn[:], op=ALU.mult)
                        pt = P()
                        TR([yn], [pt], out=pt.t[:, 0:128], in_=yn[:], identity=ident)
                        A("activation", [pt], [rbf2], out=rbf2[:], in_=pt.t[:, 0:128], func=AF.Copy)
                        S.dma(mixs[b, 4 + hp, :, (c - 2) * 128:(c - 1) * 128], rbf2[:], [rbf2], [mix_dep[b][c - 2]])

            with ExitStack() as EO:
                S.barrier()
                g1bc = sb(EO, [128, D], name="g1bc")
                bcast_row(EO, 16, b, g1bc)
                wo = sb(EO, [128, 8, D], BF16, name="wo")
                stg = sb(EO, [128, D], name="stg")
                for m in range(8):
                    S.dma(stg[:], w_out[m * 128:(m + 1) * 128, :], [], [stg])
                    G("tensor_copy", [stg], [wo], out=wo.t[:, m, :], in_=stg[:])
                mt = sb(EO, [128, 8, 128], BF16, name="mt")
                xt = sb(EO, [128, D], name="xto")
                ht = sb(EO, [128, D], name="hto")
                for c in range(16):
                    S.dma(mt[:], mixs[b].rearrange("m p t -> p m t")[:, :, c * 128:(c + 1) * 128], [mix_dep[b][c]], [mt])
                    S.dma(xt[:], xs[b, 256 + c * 128:256 + (c + 1) * 128, :], [], [xt])
                    for half in range(2):
                        hs = slice(half * 512, (half + 1) * 512)
                        ps = P()
                        for m in range(8):
                            MM([mt, wo], [ps], out=ps.t[:, :], lhsT=mt.t[:, m, :], rhs=wo.t[:, m, hs], start=(m == 0), stop=(m == 7))
                        V("tensor_tensor", [ps, g1bc], [ht.s(half)], out=ht.t[:, hs], in0=ps.t[:, :], in1=g1bc.t[:, hs], op=ALU.mult)
                        G("tensor_tensor", [ht.s(half), xt], [ht.s(half)], out=ht.t[:, hs], in0=ht.t[:, hs], in1=xt.t[:, hs], op=ALU.add)
                    S.dma(out[b, c * 128:(c + 1) * 128, :], ht[:], [ht.s(0), ht.s(1)], [out_dep[b][c]])

    EG.close()
    if "ffn" in phases:
      with ExitStack() as EF:
        S.barrier()
        w1b = sb(EF, [128, 8, 4096], BF16, name="w1b")
        w2b = sb(EF, [128, 32, D], BF16, name="w2b")
        stg = [sb(EF, [128, D], name="stgf%d" % i) for i in range(4)]
        k_ = 0
        for kc in range(8):
            for q in range(4):
                st = stg[k_ % 4]
                k_ += 1
                S.dma(st[:], w_ff1[kc * 128:(kc + 1) * 128, q * 1024:(q + 1) * 1024], [], [st])
                G("tensor_copy", [st], [w1b], out=w1b.t[:, kc, q * 1024:(q + 1) * 1024], in_=st[:])
        for fc in range(32):
            st = stg[k_ % 4]
            k_ += 1
            S.dma(st[:], w_ff2[fc * 128:(fc + 1) * 128, :], [], [st])
            G("tensor_copy", [st], [w2b], out=w2b.t[:, fc, :], in_=st[:])
        b1t = sb(EF, [128, 32], name="b1t")
        b2t = sb(EF, [128, D], name="b2t")
        fgt = sb(EF, [128, D], name="fgt")
        g2bc = sb(EF, [128, D], name="g2bc")
        S.dma(b1t[:], b1T[:, :], [], [b1t])
        S.dma(b2t[:], b2bc[:, :], [], [b2t])
        S.dma(fgt[:], fgbc[:, :], [], [fgt])
        n2T = sb(EF, [128, 8, 256], BF16, name="n2T")
        hts = [sb(EF, [128, D], name="hts%d" % i) for i in range(2)]
        wk1 = sb(EF, [128, D], name="wk1")
        wk2 = sb(EF, [128, D], name="wk2")
        rl = [sb(EF, [128, 256], name="rl%d" % i) for i in range(3)]
        h1 = [sb(EF, [128, 256], BF16, name="h1_%d" % i) for i in range(3)]
        ss = sb(EF, [128, 1], name="ssf")
        sd = sb(EF, [128, 1], name="sdf")
        rstd = sb(EF, [128, 1], name="rstdf")
        fidx = [0]

        def P4():
            p = PS[fidx[0] % 4]
            fidx[0] += 1
            return p

        for b in range(2):
            bcast_row(EF, 40, b, g2bc)
            for tt in range(8):
                for s2 in range(2):
                    c = tt * 2 + s2
                    ht = hts[s2]
                    S.dma(ht[:], out[b, c * 128:(c + 1) * 128, :], [out_dep[b][c]], [ht])
                    rms_rstd(EF, ht, wk2, ss, sd, rstd, D, "epsn")
                    V("tensor_scalar", [ht, rstd], [wk1], out=wk1[:], in0=ht[:], scalar1=rstd.t[:, 0:1], scalar2=None, op0=ALU.mult)
                    for half in range(2):
                        ps = P4()
                        for q in range(4):
                            kc = half * 4 + q
                            TR([wk1], [ps], out=ps.t[:, q * 128:(q + 1) * 128], in_=wk1.t[:, kc * 128:(kc + 1) * 128], identity=ident)
                        for q in range(4):
                            kc = half * 4 + q
                            A("activation", [ps, A2, modT], [n2T], out=n2T.t[:, kc, s2 * 128:(s2 + 1) * 128],
                              in_=ps.t[:, q * 128:(q + 1) * 128], func=AF.Identity, scale=A2.t[:, kc, b:b + 1],
                              bias=modT.t[:, 24 + kc, b:b + 1])
                acc = [PS[4], PS[5], PS[6], PS[7]]
                def ffn1(fc):
                    ps = P4()
                    for kc in range(8):
                        MM([w1b, n2T], [ps], out=ps.t[:, 0:256], lhsT=w1b.t[:, kc, fc * 128:(fc + 1) * 128], rhs=n2T.t[:, kc, :],
                           start=(kc == 0), stop=(kc == 7))
                    r_ = rl[fc % 3]
                    h_ = h1[fc % 3]
                    A("activation", [ps, b1t], [r_], out=r_[:], in_=ps.t[:, 0:256], func=AF.Relu, bias=b1t.t[:, fc:fc + 1], scale=1.0)
                    G("tensor_tensor", [r_], [h_], out=h_[:], in0=r_[:], in1=r_[:], op=ALU.mult)

                ffn1(0)
                for fc in range(32):
                    if fc + 1 < 32:
                        ffn1(fc + 1)
                    h_ = h1[fc % 3]
                    for s2 in range(2):
                        for half in range(2):
                            MM([h_, w2b], [acc[s2 * 2 + half]], out=acc[s2 * 2 + half].t[:, :], lhsT=h_.t[:, s2 * 128:(s2 + 1) * 128],
                               rhs=w2b.t[:, fc, half * 512:(half + 1) * 512], start=(fc == 0), stop=(fc == 31))
                for s2 in range(2):
                    c = tt * 2 + s2
                    for half in range(2):
                        hs = slice(half * 512, (half + 1) * 512)
                        V("tensor_tensor", [acc[s2 * 2 + half], b2t], [wk1], out=wk1.t[:, hs], in0=acc[s2 * 2 + half].t[:, :], in1=b2t.t[:, hs], op=ALU.add)
                    G("tensor_tensor", [wk1, g2bc], [wk1], out=wk1[:], in0=wk1[:], in1=g2bc[:], op=ALU.mult)
                    G("tensor_tensor", [wk1, hts[s2]], [wk1], out=wk1[:], in0=wk1[:], in1=hts[s2][:], op=ALU.add)
                    rms_rstd(EF, wk1, wk2, ss, sd, rstd, D, "epsn")
                    V("scalar_tensor_tensor", [wk1, rstd, fgt], [wk2], out=wk2[:], in0=wk1[:], scalar=rstd.t[:, 0:1], in1=fgt[:],
                      op0=ALU.mult, op1=ALU.mult)
                    S.dma(out[b, c * 128:(c + 1) * 128, :], wk2[:], [wk2], [out_dep[b][c]])
    S.finish()
    ES.close()
    nc._nins = S.nins
    return nc


_NC = None
LAST = None
NCORE = 8
BUILD_KW = {}


def kernel(**inp):
    global _NC
    f = lambda a: np.ascontiguousarray(np.asarray(a, dtype=np.float32))
    x, c, ctx, c_ctx = f(inp["x"]), f(inp["c"]), f(inp["ctx"]), f(inp["c_ctx"])
    w_in = f(inp["w_in"][0])
    perm = _partner_perm()
    qp = np.concatenate([w_in[:, h * 128 + perm] for h in range(4)], 1)
    kp = np.concatenate([w_in[:, 512 + h * 128 + perm] for h in range(4)], 1)
    cosT, sinT = _rope_tables()

    def colmajor(v, n):
        return f(np.asarray(v).reshape(n, 128).T)

    def bc(v):
        return f(np.broadcast_to(np.asarray(v).reshape(1, -1), (128, np.asarray(v).size)))

    shared = {
        "w_ada": f(inp["w_ada"][0]), "b_adaT": colmajor(inp["b_ada"][0], 48),
        "n1g": colmajor(inp["norm1_g"][0], 8), "n2g": colmajor(inp["norm2_g"][0], 8),
        "w_in": w_in, "w_perm": f(np.concatenate([qp, kp], 1)),
        "ldec": bc(inp["ret_log_decay"][0].reshape(-1)),
        "mu": f(np.asarray(inp["rwkv_shift_mu"][0]).reshape(2, 14, 128).transpose(2, 1, 0)),
        "w0T": f(np.asarray(inp["rwkv_w0"][0]).reshape(2, 4, 128).transpose(2, 1, 0)),
        "a0T": f(np.asarray(inp["rwkv_a0"][0]).reshape(2, 4, 128).transpose(2, 1, 0)),
        "w_up": f(inp["rwkv_w_up"][0]), "a_up": f(inp["rwkv_a_up"][0]), "g_up": f(inp["rwkv_g_up"][0]),
        "k_kT": colmajor(inp["rwkv_k_k"][0], 4), "k_aT": colmajor(inp["rwkv_k_a"][0], 4),
        "r_kT": colmajor(inp["rwkv_r_k"][0], 4),
        "lnw": bc(inp["rwkv_ln_w"][0]), "lnb": bc(inp["rwkv_ln_b"][0]),
        "w_out": f(inp["w_out"][0]), "w_ff1": f(inp["w_ff1"][0]), "b1T": colmajor(inp["b_ff1"][0], 32),
        "w_ff2": f(inp["w_ff2"][0]), "b2bc": bc(inp["b_ff2"][0]), "fgbc": bc(inp["final_g"]),
        "cst": _consts(), "cosT": cosT, "sinT": sinT,
    }
    in_maps = []
    ncore = NCORE
    for i in range(ncore):
        b0, b1 = 2 * i, 2 * i + 1
        xs = np.stack([np.concatenate([ctx[b0], x[b0]], 0), np.concatenate([ctx[b1], x[b1]], 0)], 0)
        cv = np.stack([c[b0], c[b1], c_ctx], 0)
        cTt = f(cv.reshape(3, 8, 128).transpose(2, 1, 0))
        m = dict(shared)
        m["xs"] = f(xs)
        m["cT"] = cTt
        in_maps.append(m)
    if _NC is None:
        try:
            _NC = build(**BUILD_KW)
        except _Stop as e_:
            _NC = e_.args[0]
    res = run_bass_kernel_spmd(_NC, in_maps, core_ids=list(range(ncore)))
    global LAST
    LAST = res.results
    outs = [np.asarray(r["out"]) for r in res.results]
    full = np.concatenate(outs, 0).astype(np.float32)
    return full
```
